# Optimizing a Trainium2 kernel written in Bass

```python
import math
import jax, jax.numpy as jnp
from jax import lax
import numpy as np

D_MODEL = 1024
BATCH = 8
SEQ = 4096
DEPTH = 4

CHUNK = 64
QBLOCK = 128
SB_HEAD_DIM = 64
SB_HEADS = D_MODEL // SB_HEAD_DIM
RET_HEADS = 4
RET_QK_DIM = D_MODEL // RET_HEADS
RET_V_DIM = 2 * RET_QK_DIM
RET_DECAY_BASE = 5.0
ROPE_BASE = 10000.0
D_FF = ((8 * D_MODEL // 3 + 127) // 128) * 128
CONV_WIDTH = 3
NORM_EPS = 1e-6
GN_EPS = 1e-5

N_SB_LAYERS = (DEPTH + 1) // 2
N_RET_LAYERS = DEPTH // 2

kernel_name = "hybrid_stickbreak_retention_convffn"


def rms_norm(x, g):
    xf = x.astype(jnp.float32)
    y = xf * lax.rsqrt(jnp.mean(xf * xf, axis=-1, keepdims=True) + NORM_EPS)
    return (y * g.astype(jnp.float32)).astype(x.dtype)


def rotary(x, pos):
    half = x.shape[-1] // 2
    inv_freq = ROPE_BASE ** (-jnp.arange(half, dtype=jnp.float32) / half)
    ang = pos[:, None] * inv_freq[None, :]
    cos = jnp.cos(ang)[None, :, None, :]
    sin = jnp.sin(ang)[None, :, None, :]
    x1, x2 = x[..., :half], x[..., half:]
    return jnp.concatenate([x1 * cos - x2 * sin, x1 * sin + x2 * cos], axis=-1)


def stick_breaking_attention(h, w_qkv, w_o):
    B, S, _ = h.shape
    qkv = (h @ w_qkv).astype(jnp.float32).reshape(B, S, 3, SB_HEADS, SB_HEAD_DIM)
    q = qkv[:, :, 0].transpose(0, 2, 1, 3)
    k = qkv[:, :, 1].transpose(0, 2, 1, 3)
    v = qkv[:, :, 2].transpose(0, 2, 1, 3)
    scale = 1.0 / math.sqrt(SB_HEAD_DIM)
    outs = []
    for blk in range(S // QBLOCK):
        start = blk * QBLOCK
        end = start + QBLOCK
        z = jnp.einsum('bhqd,bhkd->bhqk', q[:, :, start:end], k[:, :, :end]) * scale
        t_idx = start + jnp.arange(QBLOCK)
        s_idx = jnp.arange(end)
        mask = s_idx[None, :] < t_idx[:, None]
        log_one_minus = jnp.where(mask, -jax.nn.softplus(z), 0.0)
        between = lax.cumsum(log_one_minus, axis=3, reverse=True) - log_one_minus
        att = jnp.where(mask, jnp.exp(jax.nn.log_sigmoid(z) + between), 0.0)
        outs.append(jnp.einsum('bhqk,bhkd->bhqd', att, v[:, :, :end]))
    o = jnp.concatenate(outs, axis=2)
    o = o.transpose(0, 2, 1, 3).reshape(B, S, SB_HEADS * SB_HEAD_DIM)
    return o.astype(h.dtype) @ w_o


def retention(h, w_in, gn_g, w_o):
    B, S, _ = h.shape
    nc = S // CHUNK
    proj = (h @ w_in).astype(jnp.float32)
    dq = RET_HEADS * RET_QK_DIM
    dv = RET_HEADS * RET_V_DIM
    q = proj[..., :dq].reshape(B, S, RET_HEADS, RET_QK_DIM)
    k = proj[..., dq:2 * dq].reshape(B, S, RET_HEADS, RET_QK_DIM)
    v = proj[..., 2 * dq:2 * dq + dv].reshape(B, S, RET_HEADS, RET_V_DIM)
    g = proj[..., 2 * dq + dv:]
    pos = jnp.arange(S, dtype=jnp.float32)
    q = rotary(q, pos)
    k = rotary(k, pos) * (RET_QK_DIM ** -0.5)

    log_gamma = jnp.log1p(-jnp.exp2(-RET_DECAY_BASE - jnp.arange(RET_HEADS, dtype=jnp.float32)))
    i = jnp.arange(CHUNK, dtype=jnp.float32)
    d_intra = jnp.exp(log_gamma[:, None, None] * jnp.abs(i[:, None] - i[None, :]))
    q_dec = jnp.exp(log_gamma[:, None] * (i + 1.0))[None, :, :, None]
    k_dec = jnp.exp(log_gamma[:, None] * (CHUNK - 1.0 - i))[None, :, :, None]
    s_dec = jnp.exp(log_gamma * CHUNK)[None, :, None, None]

    def to_chunks(a):
        return a.reshape(B, nc, CHUNK, RET_HEADS, a.shape[-1]).transpose(1, 0, 3, 2, 4)

    qc, kc, vc = to_chunks(q), to_chunks(k), to_chunks(v)

    def step(state, inp):
        qb, kb, vb = inp
        inter = jnp.einsum('bhid,bhde->bhie', qb * q_dec, state)
        scores = jnp.einsum('bhid,bhjd->bhij', qb, kb) * d_intra[None]
        intra = jnp.einsum('bhij,bhje->bhie', scores, vb)
        new_state = s_dec * state + jnp.einsum('bhjd,bhje->bhde', kb * k_dec, vb)
        return new_state, inter + intra

    state0 = jnp.zeros((B, RET_HEADS, RET_QK_DIM, RET_V_DIM), jnp.float32)
    _, ys = lax.scan(step, state0, (qc, kc, vc))
    o = ys.transpose(1, 0, 3, 2, 4).reshape(B, S, RET_HEADS, RET_V_DIM)
    mu = jnp.mean(o, axis=-1, keepdims=True)
    var = jnp.mean(jnp.square(o - mu), axis=-1, keepdims=True)
    o = ((o - mu) * lax.rsqrt(var + GN_EPS)).reshape(B, S, dv) * gn_g.astype(jnp.float32)
    o = jax.nn.silu(g) * o
    return o.astype(h.dtype) @ w_o


def conv_ffn(h, w_up, conv_w, conv_b, w_down):
    u = h @ w_up
    u = lax.conv_general_dilated(
        u, conv_w[:, None, :].astype(u.dtype), window_strides=(1,),
        padding=[(CONV_WIDTH - 1, 0)], dimension_numbers=('NWC', 'WIO', 'NWC'),
        feature_group_count=u.shape[-1]) + conv_b
    gate, val = u[..., :D_FF], u[..., D_FF:]
    return (jax.nn.gelu(gate) * val) @ w_down


def setup_inputs(seed: int = 0) -> dict:
    key = jax.random.key(seed)
    ks = jax.random.split(key, 16)
    D = D_MODEL
    f32 = jnp.float32

    def nrm(k, shape, fan_in):
        return jax.random.normal(k, shape, f32) * (fan_in ** -0.5)

    def gain(k, shape):
        return 1.0 + 0.02 * jax.random.normal(k, shape, f32)

    ret_in_cols = 2 * RET_HEADS * RET_QK_DIM + 2 * RET_HEADS * RET_V_DIM
    return {
        "x": jax.random.normal(ks[0], (BATCH, SEQ, D), f32),
        "norm_mix_pre": gain(ks[1], (DEPTH, D)),
        "norm_mix_post": gain(ks[2], (DEPTH, D)),
        "sb_w_qkv": nrm(ks[3], (N_SB_LAYERS, D, 3 * SB_HEADS * SB_HEAD_DIM), D),
        "sb_w_o": nrm(ks[4], (N_SB_LAYERS, SB_HEADS * SB_HEAD_DIM, D), SB_HEADS * SB_HEAD_DIM),
        "ret_w_in": nrm(ks[5], (N_RET_LAYERS, D, ret_in_cols), D),
        "ret_gn": gain(ks[6], (N_RET_LAYERS, RET_HEADS * RET_V_DIM)),
        "ret_w_o": nrm(ks[7], (N_RET_LAYERS, RET_HEADS * RET_V_DIM, D), RET_HEADS * RET_V_DIM),
        "norm_ffn_pre": gain(ks[8], (DEPTH, D)),
        "norm_ffn_post": gain(ks[9], (DEPTH, D)),
        "ffn_w_up": nrm(ks[10], (DEPTH, D, 2 * D_FF), D),
        "ffn_conv_w": nrm(ks[11], (DEPTH, CONV_WIDTH, 2 * D_FF), CONV_WIDTH),
        "ffn_conv_b": 0.01 * jax.random.normal(ks[12], (DEPTH, 2 * D_FF), f32),
        "ffn_w_down": nrm(ks[13], (DEPTH, D_FF, D), D_FF),
    }


def reference(x, norm_mix_pre, norm_mix_post, sb_w_qkv, sb_w_o, ret_w_in, ret_gn, ret_w_o,
              norm_ffn_pre, norm_ffn_post, ffn_w_up, ffn_conv_w, ffn_conv_b, ffn_w_down):
    h = x
    for layer in range(DEPTH):
        hn = rms_norm(h, norm_mix_pre[layer])
        j = layer // 2
        if layer % 2 == 0:
            mix = stick_breaking_attention(hn, sb_w_qkv[j], sb_w_o[j])
        else:
            mix = retention(hn, ret_w_in[j], ret_gn[j], ret_w_o[j])
        h = h + rms_norm(mix, norm_mix_post[layer])
        hn = rms_norm(h, norm_ffn_pre[layer])
        ff = conv_ffn(hn, ffn_w_up[layer], ffn_conv_w[layer], ffn_conv_b[layer], ffn_w_down[layer])
        h = h + rms_norm(ff, norm_ffn_post[layer])
    return h
```

```python
import math
from contextlib import ExitStack

import numpy as np
import ml_dtypes

import concourse.bass as bass
import concourse.mybir as mybir
from concourse.bass_utils import run_bass_kernel_spmd

F32 = mybir.dt.float32
BF16 = mybir.dt.bfloat16
AF = mybir.ActivationFunctionType
ALU = mybir.AluOpType

D = 1024
S = 4096
NT = S // 128
DEPTH = 4
DFF = 2816
NFC = DFF // 128
EPS = 1e-6
GN_EPS = 1e-5
LAG = 2
PREFETCH_FFN = False


class Buf:
    __slots__ = ("name", "w", "r", "sem", "semval", "slot")

    def __init__(self, name):
        self.name = name
        self.w = None
        self.r = {}
        self.sem = None
        self.semval = 0
        self.slot = None


class K:
    def __init__(self, nc, n_dma_sems=64):
        self.nc = nc
        self.stack = ExitStack()
        self.engs = {"pe": nc.tensor, "act": nc.scalar, "dve": nc.vector, "pool": nc.gpsimd, "sp": nc.sync}
        self.esem, self.ecnt = {}, {}
        self.waited = {e: {} for e in self.engs}
        for e in self.engs:
            self.esem[e] = self.stack.enter_context(nc.semaphore("es_" + e))
            self.ecnt[e] = 0
        self.sempool = [[self.stack.enter_context(nc.semaphore("ds%d" % i)), 0] for i in range(n_dma_sems)]
        self.free_slots = list(range(n_dma_sems))
        self.n_inst = 0
        self.uid = 0

    def buf(self, name, dma=False):
        b = Buf(name)
        if dma:
            b.slot = self.free_slots.pop()
            b.sem, b.semval = self.sempool[b.slot]
        return b

    def release(self, b):
        if b.slot is not None:
            self.sempool[b.slot][1] = b.semval
            self.free_slots.append(b.slot)
            b.slot = None

    def _wait(self, eng, ticket):
        if ticket is None:
            return
        sem, val = ticket
        if eng == "pe" and sem is self.esem["pe"]:
            return
        key = id(sem)
        if self.waited[eng].get(key, 0) >= val:
            return
        self.waited[eng][key] = val
        self.engs[eng].wait_ge(sem, val)
        self.n_inst += 1

    def op(self, eng, fn, reads=(), writes=()):
        for b in reads:
            self._wait(eng, b.w)
        for b in writes:
            self._wait(eng, b.w)
            for t in list(b.r.values()):
                self._wait(eng, t)
        inst = fn(self.engs[eng])
        self.ecnt[eng] += 1
        sem = self.esem[eng]
        inst.then_inc(sem, 1)
        self.n_inst += 1
        t = (sem, self.ecnt[eng])
        for b in reads:
            b.r[id(sem)] = t
        for b in writes:
            b.w = t
            b.r = {}
        return t

    def dma(self, q, out, in_, sb, reads=(), writes=(), **kw):
        for b in reads:
            self._wait(q, b.w)
        for b in writes:
            if not (b.w is not None and b is sb and b.w[0] is sb.sem):
                self._wait(q, b.w)
            for t in list(b.r.values()):
                self._wait(q, t)
        inst = self.engs[q].dma_start(out=out, in_=in_, **kw)
        sb.semval += 16
        inst.then_inc(sb.sem, 16)
        self.n_inst += 1
        t = (sb.sem, sb.semval)
        for b in reads:
            b.r[id(sb.sem)] = t
        for b in writes:
            b.w = t
            b.r = {}
        return t

    def barrier(self, bufs=()):
        tickets = [(self.esem[e], self.ecnt[e]) for e in self.engs if self.ecnt[e] > 0]
        tickets += [(b.sem, b.semval) for b in bufs if b.sem is not None and b.semval > 0]
        for e in self.engs:
            for t in tickets:
                if t[0] is self.esem[e]:
                    continue
                self._wait(e, t)


class Phase:
    def __init__(self, k):
        self.k = k
        self.nc = k.nc

    def __enter__(self):
        self.st = ExitStack()
        self.bufs = []
        return self

    def T(self, name, shape, dt, dma=False):
        self.k.uid += 1
        t = self.st.enter_context(self.nc.sbuf_tensor("%s_%d" % (name, self.k.uid), list(shape), dt))
        b = self.k.buf(name, dma)
        self.bufs.append(b)
        return t, b

    def __exit__(self, *a):
        self.k.barrier(self.bufs)
        for b in self.bufs:
            self.k.release(b)
        self.st.close()
        return False


def _consts():
    bf = ml_dtypes.bfloat16
    c = {}
    c["ident"] = np.eye(128, dtype=np.float32).astype(bf)
    j = np.arange(128)[:, None]
    s = np.arange(128)[None, :]
    c["negtri"] = (-(j >= s).astype(np.float32)).astype(bf)
    bp = np.arange(32)[:, None, None]
    b = np.arange(32)[None, :, None]
    selk = np.zeros((64, S), np.float32)
    selk[0:32] = -(np.arange(32)[:, None] > (np.arange(S)[None, :] // 128)).astype(np.float32)
    c["selk"] = selk.astype(bf)
    c["zero"] = np.zeros((64, S), np.float32).astype(bf)
    es = np.zeros((128, 32, 128), np.float32)
    for i in range(32):
        es[:, i, i] = 1.0
        es[:, i, 64 + i] = 1.0
    c["esel"] = es.reshape(128, 32 * 128).astype(bf)
    p = np.arange(128)[:, None]
    xx = np.arange(512)[None, :]
    c["masks"] = np.concatenate([((xx - 128 * kk - p) > 0).astype(np.float32) for kk in range(4)], 1).astype(bf)
    half = 128
    inv_freq = (np.float32(10000.0) ** (-np.arange(half, dtype=np.float32) / np.float32(half))).astype(np.float32)
    pos = np.arange(S, dtype=np.float32)
    ang = (pos[None, :] * inv_freq[:, None]).astype(np.float32)
    c["cosT"] = np.cos(ang.astype(np.float64)).astype(np.float32)
    c["sinT"] = np.sin(ang.astype(np.float64)).astype(np.float32)
    i = np.arange(128)
    dpt = np.zeros((128, 4, 128), np.float64)
    rc = np.zeros((128, 16), np.float64)
    sdec = []
    for h in range(4):
        lg = np.log1p(-np.exp2(-5.0 - h))
        a = np.exp(lg * (i + 1.0))
        ci, cj = i[:, None] // 64, i[None, :] // 64
        li, lj = i[:, None] % 64, i[None, :] % 64
        Dm = np.where(ci == cj, np.exp(lg * np.abs(li - lj)),
                      np.where(ci > cj, np.exp(lg * (i[:, None] - i[None, :]).clip(0)), 0.0))
        dpt[:, h, :] = (Dm / a[:, None]).T * (256.0 ** -0.5)
        rc[:, h] = a
        rc[:, 4 + h] = a * a
        rc[:, 8 + h] = np.exp(lg * (127.0 - i)) * (256.0 ** -0.5)
        sdec.append(float(np.exp(lg * 128.0)))
    c["dpt"] = dpt.reshape(128, 512).astype(np.float32)
    c["rcol"] = rc.astype(np.float32)
    return c, sdec


_SDEC = [float(np.exp(np.log1p(-np.exp2(-5.0 - h)) * 128.0)) for h in range(4)]


def build(n_sub=None, dbg=False, layers=(0, 1, 2, 3)):
    nc = bass.Bass("TRN2", target_bir_lowering=False)

    def din(name, shape, dt=F32):
        return nc.dram_tensor(name, list(shape), dt, kind="ExternalInput").ap()

    def dscr(name, shape, dt):
        if dbg:
            return nc.dram_tensor(name, list(shape), dt, kind="ExternalOutput").ap()
        return nc.dram_tensor(name, list(shape), dt).ap()

    x = din("x", [S, D])
    g_mix_pre = din("norm_mix_pre", [DEPTH, D])
    g_mix_post = din("norm_mix_post", [DEPTH, D])
    g_ffn_pre = din("norm_ffn_pre", [DEPTH, D])
    g_ffn_post = din("norm_ffn_post", [DEPTH, D])
    sb_w_qkv = din("sb_w_qkv", [2, D, 3072])
    sb_w_o = din("sb_w_o", [2, D, D])
    ret_w_in = din("ret_w_in", [2, D, 6144])
    ret_gn = din("ret_gn", [2, 128, 16])
    ret_w_o = din("ret_w_o", [2, 2048, D])
    ffn_w_up = din("ffn_w_up", [DEPTH, D, 2 * DFF])
    ffn_cw = din("ffn_cw", [DEPTH, 128, 3, 44])
    ffn_cb = din("ffn_cb", [DEPTH, 128, 44])
    ffn_w_down = din("ffn_w_down", [DEPTH, DFF, D])
    c_ident = din("c_ident", [128, 128], BF16)
    c_negtri = din("c_negtri", [128, 128], BF16)
    c_selk = din("c_selk", [64, S], BF16)
    c_zero = din("c_zero", [64, S], BF16)
    c_esel = din("c_esel", [128, 32 * 128], BF16)
    c_masks = din("c_masks", [128, 4 * 512], BF16)
    c_cosT = din("c_cosT", [128, S])
    c_sinT = din("c_sinT", [128, S])
    c_dpt = din("c_dpt", [128, 512])
    c_rcol = din("c_rcol", [128, 16])
    y = nc.dram_tensor("y", [S, D], F32, kind="ExternalOutput").ap()

    hs = dscr("hs", [S, D], F32)
    qT_d = dscr("qT_d", [8, 128, S], BF16)
    kT_d = dscr("kT_d", [8, 128, S], BF16)
    v_d = dscr("v_d", [S, 2048], BF16)
    g_d = dscr("g_d", [S, 2048], BF16)
    oT_d = dscr("oT_d", [8, 128, 16, 512], BF16)

    k = K(nc)
    gs = k.stack
    ps = gs.enter_context(nc.psum_tensor("ps", [128, 8, 512], F32))
    PB = [k.buf("psb%d" % i) for i in range(8)]
    ident = gs.enter_context(nc.sbuf_tensor("ident", [128, 128], BF16))
    identB = k.buf("ident", dma=True)
    k.dma("sp", ident[:], c_ident[:, :], identB, writes=[identB])

    nhalf = gs.enter_context(nc.sbuf_tensor("nhalf", [128, 1], F32))
    nhalfB = k.buf("nhalf")
    k.op("pool", lambda e: e.memset(nhalf[:], -0.5), [], [nhalfB])

    def rstd_pool(out_ap, in_ap, buf, scale):
        k.op("pool", lambda e: e.tensor_scalar(out=out_ap, in0=in_ap, scalar1=scale, scalar2=EPS,
                                               op0=ALU.mult, op1=ALU.add), [buf], [buf])
        k.op("pool", lambda e: e.tensor_tensor(out=out_ap, in0=out_ap, in1=nhalf[:], op=ALU.pow),
             [buf, nhalfB], [buf])

    def psbf(bank):
        return ps[:, bank, :].bitcast(BF16)

    sub = [0]

    def want():
        sub[0] += 1
        return n_sub is None or sub[0] <= n_sub

    def load_w(P, name, src, nchunk, ncols, col0=0):
        t, b = P.T(name, [128, nchunk, ncols], BF16, dma=True)
        v = src.rearrange("(c p) n -> p c n", p=128)
        for c in range(nchunk):
            k.dma("pool", t[:, c, :], v[:, c, col0:col0 + ncols], b, writes=[b], max_dma_last_dim=4096)
        return t, b

    def load_w_blocks(P, name, src, nchunk, blocks):
        out = []
        v = src.rearrange("(c p) n -> p c n", p=128)
        for (col0, ncols) in blocks:
            t, b = P.T(name, [128, nchunk, ncols], BF16, dma=True)
            k.dma("pool", t[:], v[:, :, col0:col0 + ncols], b, writes=[b], max_dma_last_dim=4096)
            out.append((t, b))
        return out

    def load_bcast(P, name, vec):
        t, b = P.T(name, [128, D], F32, dma=True)
        k.dma("sp", t[:], vec.partition_broadcast(128), b, writes=[b])
        return t, b

    class Front:
        def __init__(self, P, hsrc, gvec, trbank, hb_n=1, nt=4, halo=0, use_pool=False):
            self.P, self.hsrc, self.trbank, self.nt, self.halo = P, hsrc, trbank, nt, halo
            self.use_pool = use_pool
            self.hb = [P.T("hb", [128, nt, D], F32, dma=True) for _ in range(hb_n)]
            self.hnT = [P.T("hnT", [128, 8, halo + nt * 128], BF16) for _ in range(2)]
            self.hnb = [P.T("hnb", [128, D], BF16) for _ in range(nt)]
            self.junk = P.T("junk", [128, D], BF16)
            self.stat = [P.T("stat", [128, 4], F32) for _ in range(4)]
            self.g = load_bcast(P, "gpre", gvec)

        def load(self, st):
            t, b = self.hb[st % len(self.hb)]
            src = self.hsrc.rearrange("(t p) n -> p t n", p=128)[:, st * self.nt:(st + 1) * self.nt, :]
            k.dma("sp", t[:], src, b, writes=[b])

        def norm_a(self, st, t):
            hb_t, hb_b = self.hb[st % len(self.hb)]
            jk_t, jk_b = self.junk
            sm, smb = self.stat[t]
            k.op("act", lambda e: e.activation(out=jk_t[:], in_=hb_t[:, t, :], func=AF.Square,
                                               accum_out=sm[:, 0:1]), [hb_b], [jk_b, smb])

        def norm_b(self, st, t):
            sm, smb = self.stat[t]
            if self.use_pool:
                rstd_pool(sm[:, 2:3], sm[:, 0:1], smb, 1.0 / D)
            else:
                k.op("act", lambda e: e.activation(out=sm[:, 1:2], in_=sm[:, 0:1], func=AF.Sqrt,
                                                   scale=1.0 / D, bias=EPS), [smb], [smb])
                k.op("dve", lambda e: e.reciprocal(out=sm[:, 2:3], in_=sm[:, 1:2]), [smb], [smb])

        def norm_c(self, st, t):
            hb_t, hb_b = self.hb[st % len(self.hb)]
            g_t, g_b = self.g
            sm, smb = self.stat[t]
            hn, hnb_ = self.hnb[t]
            k.op("dve", lambda e: e.scalar_tensor_tensor(out=hn[:], in0=hb_t[:, t, :], scalar=sm[:, 2:3],
                                                         in1=g_t[:], op0=ALU.mult, op1=ALU.mult),
                 [hb_b, smb, g_b], [hnb_])

        def emit_norm(self, st, tiles=None):
            for t in (range(self.nt) if tiles is None else tiles):
                self.norm_a(st, t)
                self.norm_b(st, t)
                self.norm_c(st, t)

        def tr_begin(self, st):
            hnT_t, hnT_b = self.hnT[st % 2]
            H = self.halo
            if H:
                if st == 0:
                    k.op("dve", lambda e: e.memset(hnT_t[:, :, 0:H], 0.0), [], [hnT_b])
                else:
                    p_t, p_b = self.hnT[(st - 1) % 2]
                    k.op("dve", lambda e: e.tensor_copy(out=hnT_t[:, :, 0:H],
                                                         in_=p_t[:, :, self.nt * 128:self.nt * 128 + H]),
                         [p_b], [hnT_b])

        def tr_tile(self, st, t):
            hb_t, hb_b = self.hb[st % len(self.hb)]
            hnT_t, hnT_b = self.hnT[st % 2]
            trp = psbf(self.trbank)
            H = self.halo
            hn, hnb_ = self.hnb[t]
            for c in range(8):
                k.op("pe", lambda e: e.transpose(out=trp[:, c * 128:(c + 1) * 128],
                                                 in_=hn[:, c * 128:(c + 1) * 128], identity=ident[:]),
                     [hnb_, identB], [PB[self.trbank]])
            k.op("dve", lambda e: e.tensor_copy(out=hnT_t[:, :, H + t * 128:H + (t + 1) * 128],
                                                in_=trp.rearrange("p (c n) -> p c n", c=8)),
                 [PB[self.trbank]], [hnT_b])
            return hnT_t, hnT_b, hb_t, hb_b

        def emit_tr(self, st):
            self.tr_begin(st)
            for t in range(self.nt):
                r = self.tr_tile(st, t)
            return r

        def emit(self, st):
            self.emit_norm(st)
            return self.emit_tr(st)

    class Post:
        def __init__(self, P, gvec, ntmp=2, split_add=False, use_pool=False):
            self.split_add = split_add
            self.use_pool = use_pool
            self.g = load_bcast(P, "gpost", gvec)
            self.junk = P.T("pjunk", [128, 512], BF16)
            self.tmp = [P.T("ptmp", [128, D], F32) for _ in range(ntmp)]
            self.stat = [P.T("pstat", [128, 8], F32) for _ in range(2)]
            self.n = 0

        def stage_a(self, banks):
            jk, jkb = self.junk
            tmp, tmpb = self.tmp[self.n % len(self.tmp)]
            sm, smb = self.stat[self.n % 2]
            self.n += 1
            for hf in range(2):
                k.op("act", lambda e: e.activation(out=jk[:], in_=ps[:, banks[hf], :], func=AF.Square,
                                                   accum_out=sm[:, hf:hf + 1]), [PB[banks[hf]]], [jkb, smb])
            return {"banks": banks, "tmp": tmp, "tmpb": tmpb, "sm": sm, "smb": smb}

        def stage_b(self, c):
            sm, smb = c["sm"], c["smb"]
            if self.use_pool:
                k.op("pool", lambda e: e.tensor_tensor(out=sm[:, 2:3], in0=sm[:, 0:1], in1=sm[:, 1:2], op=ALU.add),
                     [smb], [smb])
                rstd_pool(sm[:, 4:5], sm[:, 2:3], smb, 1.0 / D)
            else:
                k.op("dve", lambda e: e.tensor_tensor(out=sm[:, 2:3], in0=sm[:, 0:1], in1=sm[:, 1:2], op=ALU.add),
                     [smb], [smb])
                k.op("act", lambda e: e.activation(out=sm[:, 3:4], in_=sm[:, 2:3], func=AF.Sqrt,
                                                   scale=1.0 / D, bias=EPS), [smb], [smb])
                k.op("dve", lambda e: e.reciprocal(out=sm[:, 4:5], in_=sm[:, 3:4]), [smb], [smb])

        def stage_c(self, c, h_ap, h_b):
            g_t, g_b = self.g
            banks, tmp, tmpb, sm, smb = c["banks"], c["tmp"], c["tmpb"], c["sm"], c["smb"]
            for hf in range(2):
                k.op("dve", lambda e: e.scalar_tensor_tensor(out=tmp[:, hf * 512:(hf + 1) * 512],
                                                             in0=ps[:, banks[hf], :], scalar=sm[:, 4:5],
                                                             in1=g_t[:, hf * 512:(hf + 1) * 512],
                                                             op0=ALU.mult, op1=ALU.mult),
                     [PB[banks[hf]], smb, g_b], [tmpb])
            if self.split_add:
                k.op("pool", lambda e: e.tensor_tensor(out=h_ap[:, 0:512], in0=h_ap[:, 0:512], in1=tmp[:, 0:512],
                                                       op=ALU.add), [tmpb, h_b], [h_b])
                k.op("dve", lambda e: e.tensor_tensor(out=h_ap[:, 512:D], in0=h_ap[:, 512:D], in1=tmp[:, 512:D],
                                                      op=ALU.add), [tmpb, h_b], [h_b])
            else:
                k.op("pool", lambda e: e.tensor_tensor(out=h_ap, in0=h_ap, in1=tmp[:], op=ALU.add),
                     [tmpb, h_b], [h_b])

        def emit(self, banks, h_ap, h_b):
            c = self.stage_a(banks)
            self.stage_b(c)
            self.stage_c(c, h_ap, h_b)

    def phase_sb_proj(layer, hsrc):
        j = layer // 2
        with Phase(k) as P:
            wblk = load_w_blocks(P, "wqkv", sb_w_qkv[j], 8, [(0, D), (D, D), (2 * D, D)])
            fr = Front(P, hsrc, g_mix_pre[layer], 7)
            qst = [P.T("qst", [128, 8, 512], BF16, dma=True) for _ in range(2)]
            kst = [P.T("kst", [128, 8, 512], BF16, dma=True) for _ in range(2)]
            vst = [P.T("vst", [128, 4, D], BF16, dma=True) for _ in range(2)]
            fr.load(0)
            bank = [0]

            def nb():
                bank[0] = (bank[0] + 1) % 6
                return bank[0]

            ev = [0]
            cur = fr.emit(0)
            fr.load(1)
            for st in range(8):
                hnT, hnTb, _, _ = cur
                cs = slice(st * 512, (st + 1) * 512)
                for which, (stg, dst, scale) in enumerate(((qst, qT_d, 0.125), (kst, kT_d, 1.0))):
                    s_t, s_b = stg[st % 2]
                    w, wb = wblk[which]
                    for hp in range(8):
                        bk = nb()
                        for c in range(8):
                            k.op("pe", lambda e: e.matmul(ps[:, bk, :],
                                                          lhsT=w[:, c, hp * 128:(hp + 1) * 128],
                                                          rhs=hnT[:, c, :], start=(c == 0), stop=(c == 7)),
                                 [wb, hnTb], [PB[bk]])
                        ev[0] += 1
                        if ev[0] % 2:
                            k.op("act", lambda e: e.mul(out=s_t[:, hp, :], in_=ps[:, bk, :], mul=scale),
                                 [PB[bk]], [s_b])
                        else:
                            k.op("dve", lambda e: e.tensor_scalar_mul(out=s_t[:, hp, :], in0=ps[:, bk, :],
                                                                      scalar1=scale), [PB[bk]], [s_b])
                    k.dma("sp", dst.rearrange("h p t -> p h t")[:, :, cs], s_t[:], s_b, reads=[s_b])
                    if st + 1 < 8 and which == 0:
                        fr.emit_norm(st + 1)
                if st + 1 < 8:
                    cur = fr.emit_tr(st + 1)
                    if st + 2 < 8:
                        fr.load(st + 2)
                v_t, v_b = vst[st % 2]
                w, wb = wblk[2]
                for t in range(4):
                    for hf in range(2):
                        bk = nb()
                        for c in range(8):
                            k.op("pe", lambda e: e.matmul(ps[:, bk, :], lhsT=hnT[:, c, t * 128:(t + 1) * 128],
                                                          rhs=w[:, c, hf * 512:(hf + 1) * 512],
                                                          start=(c == 0), stop=(c == 7)), [wb, hnTb], [PB[bk]])
                        ev[0] += 1
                        eng = "act" if ev[0] % 2 else "dve"
                        if eng == "act":
                            k.op("act", lambda e: e.copy(out=v_t[:, t, hf * 512:(hf + 1) * 512], in_=ps[:, bk, :]),
                                 [PB[bk]], [v_b])
                        else:
                            k.op("dve", lambda e: e.tensor_copy(out=v_t[:, t, hf * 512:(hf + 1) * 512],
                                                                in_=ps[:, bk, :]), [PB[bk]], [v_b])
                k.dma("sp", v_d.rearrange("(t p) n -> p t n", p=128)[:, st * 4:(st + 1) * 4, 0:D], v_t[:], v_b,
                      reads=[v_b])

    def phase_sb_attn():
        with Phase(k) as P:
            tri, trib = P.T("tri", [128, 128], BF16, dma=True)
            esel, eselb = P.T("esel", [128, 32 * 128], BF16, dma=True)
            msk, mskb = P.T("msk", [128, 4 * 512], BF16, dma=True)
            k.dma("sp", tri[:], c_negtri[:, :], trib, writes=[trib])
            k.dma("sp", esel[:], c_esel[:, :], eselb, writes=[eselb])
            k.dma("sp", msk[:], c_masks[:, :], mskb, writes=[mskb])
            qp = [[P.T("qp", [128, S], BF16, dma=True) for _e in range(2)] for _ in range(2)]
            kp = [[P.T("kp", [128, S], BF16, dma=True) for _e in range(2)] for _ in range(2)]
            vp = [P.T("vp", [128, NT, 128], BF16, dma=True) for _ in range(2)]
            oT = [P.T("oT", [128, S], BF16, dma=True) for _ in range(2)]
            NSLOT = 48
            Ls = [P.T("Ls%d" % i, [128, NSLOT, 512], BF16)[0] for i in range(2)]
            LB = [[k.buf("L%d_%d" % (i, b_)) for b_ in range(NSLOT)] for i in range(2)]
            Asb = [P.T("Asb", [128, 512], BF16) for _ in range(3)]
            oh = [slice(64, 128), slice(0, 64)]
            hh = [slice(0, 64), slice(64, 128)]
            for i_ in range(2):
                for e_ in range(2):
                    k.dma("sp", kp[i_][e_][0][oh[e_], :], c_selk[:, :], kp[i_][e_][1], writes=[kp[i_][e_][1]])

            def load(hp):
                for e_ in range(2):
                    q_t, q_b = qp[hp % 2][e_]
                    k_t, k_b = kp[hp % 2][e_]
                    k.dma("sp", q_t[hh[e_], :], qT_d[hp, hh[e_], :], q_b, writes=[q_b])
                    k.dma("sp", q_t[oh[e_], :], c_zero[:, :], q_b, writes=[q_b])
                    k.dma("sp", k_t[hh[e_], :], kT_d[hp, hh[e_], :], k_b, writes=[k_b])
                k.dma("sp", vp[hp % 2][0][:],
                      v_d.rearrange("(t p) n -> p t n", p=128)[:, :, hp * 128:(hp + 1) * 128],
                      vp[hp % 2][1], writes=[vp[hp % 2][1]])

            load(0)
            cnt = {"z": 0, "p": 0, "a": 0}
            items = []

            def pass1(hp, J, e_, off):
                nb = 4 * J + 4
                cs = slice(J * 512, (J + 1) * 512)
                q_t, q_b = qp[hp % 2][e_]
                k_t, k_b = kp[hp % 2][e_]
                L, Lb = Ls[e_], LB[e_]
                for i in range(nb):
                    st_ = {}

                    def produce(i=i, st_=st_):
                        zb = cnt["z"] % 3
                        cnt["z"] += 1
                        st_["b"] = zb
                        c0 = 128 * max(0, i - 4 * J) if hp > 0 else 0
                        k.op("pe", lambda e: e.matmul(ps[:, zb, c0:512], lhsT=k_t[:, i * 128:(i + 1) * 128],
                                                      rhs=q_t[:, J * 512 + c0:(J + 1) * 512], start=True, stop=True),
                             [k_b, q_b], [PB[zb]])

                    def consume(jb=i, st_=st_):
                        zb = st_["b"]
                        c0 = 128 * max(0, jb - 4 * J) if hp > 0 else 0
                        k.op("act", lambda e: e.activation(out=L[:, off + jb, c0:512], in_=ps[:, zb, c0:512],
                                                           func=AF.Softplus), [PB[zb]], [Lb[off + jb]])
                        kk = jb - 4 * J
                        if kk >= 0:
                            k.op("dve", lambda e: e.tensor_tensor(out=L[:, off + jb, :], in0=L[:, off + jb, :],
                                                                  in1=msk[:, kk * 512:(kk + 1) * 512], op=ALU.mult),
                                 [mskb, Lb[off + jb]], [Lb[off + jb]])
                        k.op("pe", lambda e: e.matmul(ps[:, 6, :], lhsT=esel[:, jb * 128:(jb + 1) * 128],
                                                      rhs=L[:, off + jb, :], start=(jb == 0), stop=(jb == nb - 1)),
                             [eselb, Lb[off + jb]], [PB[6]])
                        if jb == nb - 1:
                            so = 64 if e_ == 0 else 0
                            k.op("dve", lambda e: e.tensor_copy(out=q_t[so:so + 32, cs], in_=ps[so:so + 32, 6, :]),
                                 [PB[6]], [q_b])

                    items.append((produce, consume))

            def pass2(hp, J, e_, off, final=False):
                nb = 4 * J + 4
                cs = slice(J * 512, (J + 1) * 512)
                q_t, q_b = qp[hp % 2][e_]
                k_t, k_b = kp[hp % 2][e_]
                v_t, v_b = vp[hp % 2]
                o_t, o_b = oT[hp % 2]
                L, Lb = Ls[e_], LB[e_]
                for i in range(nb):
                    st_ = {}

                    def produce(i=i, st_=st_):
                        pb = 3 + cnt["p"] % 3
                        cnt["p"] += 1
                        st_["b"] = pb
                        c0 = 128 * max(0, i - 4 * J) if hp > 0 else 0
                        k.op("pe", lambda e: e.matmul(ps[:, pb, c0:512], lhsT=k_t[:, i * 128:(i + 1) * 128],
                                                      rhs=q_t[:, J * 512 + c0:(J + 1) * 512], start=True, stop=False),
                             [k_b, q_b], [PB[pb]])
                        k.op("pe", lambda e: e.matmul(ps[:, pb, c0:512], lhsT=tri[:], rhs=L[:, off + i, c0:512],
                                                      start=False, stop=True), [trib, Lb[off + i]], [PB[pb]])

                    def consume(jb=i, st_=st_):
                        pb = st_["b"]
                        a_t, a_b = Asb[cnt["a"] % 3]
                        cnt["a"] += 1
                        c0 = 128 * max(0, jb - 4 * J) if hp > 0 else 0
                        k.op("act", lambda e: e.activation(out=a_t[:, c0:512], in_=ps[:, pb, c0:512], func=AF.Exp),
                             [PB[pb]], [a_b])
                        kk = jb - 4 * J
                        if kk >= 0:
                            k.op("dve", lambda e: e.tensor_tensor(out=a_t[:], in0=a_t[:],
                                                                  in1=msk[:, kk * 512:(kk + 1) * 512], op=ALU.mult),
                                 [mskb, a_b], [a_b])
                        k.op("pe", lambda e: e.matmul(ps[:, 7, :], lhsT=v_t[:, jb, :], rhs=a_t[:],
                                                      start=(jb == 0), stop=(jb == nb - 1)), [v_b, a_b], [PB[7]])
                        if jb == nb - 1:
                            k.op("dve", lambda e: e.tensor_copy(out=o_t[hh[e_], cs], in_=ps[hh[e_], 7, :]),
                                 [PB[7]], [o_b])
                            if final:
                                k.dma("sp", oT_d.rearrange("g p c t -> p g c t")[:, :, hp, :],
                                      o_t.rearrange("p (g t) -> p g t", g=8), o_b, reads=[o_b])
                                if hp + 2 < 8:
                                    load(hp + 2)

                    items.append((produce, consume))

            load(1)
            jsets = [((7, 0), (3, 32)), ((6, 0), (4, 28)), ((5, 0), (2, 24), (1, 36), (0, 44))]
            for hp in range(8):
                for si, js in enumerate(jsets):
                    for (J, off) in js:
                        pass1(hp, J, 0, off)
                        pass1(hp, J, 1, off)
                    for ji, (J, off) in enumerate(js):
                        pass2(hp, J, 0, off)
                        pass2(hp, J, 1, off, final=(si == len(jsets) - 1 and ji == len(js) - 1))
            for n in range(len(items) + LAG):
                if n < len(items):
                    items[n][0]()
                if n - LAG >= 0:
                    items[n - LAG][1]()

    def phase_outproj(layer, w_src, kc, hsrc, hdst, background=()):
        background = list(background)
        with Phase(k) as P:
            w, wb = load_w(P, "wo", w_src, kc, D)
            post = Post(P, g_mix_post[layer], split_add=True)
            yT = [P.T("yT", [128, kc, 512], BF16, dma=True) for _ in range(2)]
            hb = [P.T("hbo", [128, 4, D], F32, dma=True) for _ in range(2)]

            def load(g):
                cs = slice(g * 512, (g + 1) * 512)
                k.dma("sp", yT[g % 2][0][:], oT_d[g, :, 0:kc, :], yT[g % 2][1], writes=[yT[g % 2][1]])
                k.dma("sp", hb[g % 2][0][:], hsrc.rearrange("(t p) n -> p t n", p=128)[:, g * 4:(g + 1) * 4, :],
                      hb[g % 2][1], writes=[hb[g % 2][1]])

            load(0)
            bc = 0
            for g in range(8):
                if g + 1 < 8:
                    load(g + 1)
                y_t, y_b = yT[g % 2]
                h_t, h_b = hb[g % 2]
                for t in range(4):
                    banks = []
                    for hf in range(2):
                        bk = bc % 8
                        bc += 1
                        banks.append(bk)
                        for c in range(kc):
                            k.op("pe", lambda e: e.matmul(ps[:, bk, :], lhsT=y_t[:, c, t * 128:(t + 1) * 128],
                                                          rhs=w[:, c, hf * 512:(hf + 1) * 512],
                                                          start=(c == 0), stop=(c == kc - 1)), [wb, y_b], [PB[bk]])
                    post.emit(banks, h_t[:, t, :], h_b)
                    if background:
                        background.pop(0)()
                k.dma("sp", hdst.rearrange("(t p) n -> p t n", p=128)[:, g * 4:(g + 1) * 4, :], h_t[:], h_b,
                      reads=[h_b])
            while background:
                background.pop(0)()

    def ffn_weight_loaders(PW, layer, with_down):
        wu, wub = PW.T("wup", [128, 8, 2 * DFF], BF16, dma=True)
        vu = ffn_w_up[layer].rearrange("(c p) n -> p c n", p=128)
        loaders = [(lambda c=c: k.dma("pool", wu[:, c, :], vu[:, c, :], wub, writes=[wub], max_dma_last_dim=4096))
                   for c in range(8)]
        pre = {"wu": (wu, wub)}
        if with_down:
            wd, wdb = PW.T("wdn", [128, NFC, D], BF16, dma=True)
            vd = ffn_w_down[layer].rearrange("(c p) n -> p c n", p=128)
            loaders += [(lambda c=c: k.dma("pool", wd[:, c, :], vd[:, c, :], wdb, writes=[wdb],
                                           max_dma_last_dim=4096)) for c in range(NFC)]
            pre["wd"] = (wd, wdb)
        return pre, loaders

    def phase_ffn(layer, hsrc, hdst, pre=None):
        TS, NTS, NST = 256, 2, 16
        pre = pre or {}
        with Phase(k) as P:
            ublocks = []
            for i_ in range(6):
                wcols = min(512, DFF - i_ * 512)
                ublocks += [(i_ * 512, wcols), (DFF + i_ * 512, wcols)]
            wub_l = load_w_blocks(P, "wup", ffn_w_up[layer], 8, ublocks)
            wd, wdb = pre["wd"] if "wd" in pre else load_w(P, "wdn", ffn_w_down[layer], NFC, D)
            fr = Front(P, hsrc, g_ffn_pre[layer], 7, hb_n=1, nt=NTS, halo=2, use_pool=True)
            post = Post(P, g_ffn_post[layer], ntmp=1, use_pool=True)
            cw, cwb = P.T("cw", [128, 3, 44], F32, dma=True)
            cb, cbb = P.T("cb", [128, 44], F32, dma=True)
            k.dma("sp", cw[:], ffn_cw[layer], cwb, writes=[cwb])
            k.dma("sp", cb[:], ffn_cb[layer], cbb, writes=[cbb])
            cv = [P.T("cv", [128, TS], F32) for _ in range(6)]
            gl = [P.T("gl", [128, TS], F32) for _ in range(3)]
            actTs = [P.T("actT", [128, NFC, TS], BF16) for _ in range(2)]
            hr = [P.T("hr", [128, D], F32, dma=True) for _ in range(2)]
            hsrc_t = hsrc.rearrange("(t p) n -> p t n", p=128)
            hdst_t = hdst.rearrange("(t p) n -> p t n", p=128)
            fr.load(0)
            bc = [0]
            uc = 0
            hrc = [0]
            events = {}

            def at(step, fn):
                events.setdefault(step, []).append(fn)

            def run_events(step):
                for fn in events.pop(step, []):
                    fn()

            class Down:
                def __init__(self, st_, actT, actTb):
                    self.st, self.actT, self.actTb, self.m, self.banks = st_, actT, actTb, 0, {}

                def emit(self, step, n):
                    for _ in range(n):
                        if self.m >= 4 * NFC:
                            return
                        gi, jc = self.m // NFC, self.m % NFC
                        t, hf = gi // 2, gi % 2
                        if jc == 0:
                            self.banks[gi] = 4 + bc[0] % 3
                            bc[0] += 1
                        bk = self.banks[gi]
                        actT, actTb = self.actT, self.actTb
                        k.op("pe", lambda e: e.matmul(ps[:, bk, :], lhsT=actT[:, jc, t * 128:(t + 1) * 128],
                                                      rhs=wd[:, jc, hf * 512:(hf + 1) * 512],
                                                      start=(jc == 0), stop=(jc == NFC - 1)),
                             [wdb, actTb], [PB[bk]])
                        self.m += 1
                        if jc == NFC - 1 and hf == 1:
                            self.schedule_post(step, t, (self.banks[gi - 1], self.banks[gi]))

                def schedule_post(self, step, t, banks):
                    tile = self.st * NTS + t
                    h_t, h_b = hr[hrc[0] % 2]
                    hrc[0] += 1
                    ctx = {}

                    def s_a():
                        k.dma("sp", h_t[:], hsrc_t[:, tile, :], h_b, writes=[h_b])
                        ctx["c"] = post.stage_a(banks)

                    def s_b():
                        post.stage_b(ctx["c"])

                    def s_c():
                        post.stage_c(ctx["c"], h_t[:], h_b)
                        k.dma("sp", hdst_t[:, tile, :], h_t[:], h_b, reads=[h_b])

                    at(step + 1, s_a)
                    at(step + 3, s_b)
                    at(step + 4, s_c)

            dn = None
            cur = fr.emit(0)
            for st in range(NST):
                hnT, hnTb, _, _ = cur
                actT, actTb = actTs[st % 2]
                nxt = st + 1 < NST
                for jc in range(NFC):
                    step = st * NFC + jc
                    if dn is not None:
                        dn.emit(step, 4)
                    run_events(step)
                    if nxt:
                        if jc == 2:
                            fr.load(st + 1)
                        if jc == 5:
                            fr.tr_begin(st + 1)
                        if jc == 6:
                            fr.norm_a(st + 1, 0)
                        if jc == 8:
                            fr.norm_a(st + 1, 1)
                            fr.norm_b(st + 1, 0)
                        if jc == 9:
                            fr.norm_c(st + 1, 0)
                        if jc == 10:
                            fr.norm_b(st + 1, 1)
                        if jc == 12:
                            fr.norm_c(st + 1, 1)
                        if jc == 15:
                            fr.tr_tile(st + 1, 0)
                        if jc == 18:
                            cur = fr.tr_tile(st + 1, 1)
                    pair = []
                    for gv in range(2):
                        bk = (uc % 4)
                        uc += 1
                        ch = gv * NFC + jc
                        wu, wub = wub_l[(jc // 4) * 2 + gv]
                        col = (jc % 4) * 128
                        for c in range(8):
                            k.op("pe", lambda e: e.matmul(ps[:, bk, 0:TS + 2], lhsT=wu[:, c, col:col + 128],
                                                          rhs=hnT[:, c, :], start=(c == 0), stop=(c == 7)),
                                 [wub, hnTb], [PB[bk]])
                        c_t, c_b = cv[uc % len(cv)]
                        k.op("act", lambda e: e.activation(out=c_t[:], in_=ps[:, bk, 2:TS + 2], func=AF.Identity,
                                                           scale=cw[:, 2, ch:ch + 1], bias=cb[:, ch:ch + 1]),
                             [PB[bk], cwb, cbb], [c_b])
                        for kk in (1, 0):
                            k.op("dve", lambda e: e.scalar_tensor_tensor(out=c_t[:], in0=ps[:, bk, kk:kk + TS],
                                                                         scalar=cw[:, kk, ch:ch + 1], in1=c_t[:],
                                                                         op0=ALU.mult, op1=ALU.add),
                                 [PB[bk], cwb, c_b], [c_b])
                        pair.append((c_t, c_b))
                    g_t, g_b = gl[jc % len(gl)]
                    k.op("act", lambda e: e.activation(out=g_t[:], in_=pair[0][0][:], func=AF.Gelu_apprx_tanh),
                         [pair[0][1]], [g_b])
                    k.op("pool", lambda e: e.tensor_tensor(out=actT[:, jc, :], in0=g_t[:], in1=pair[1][0][:],
                                                           op=ALU.mult), [g_b, pair[1][1]], [actTb])
                dn = Down(st, actT, actTb)
            step = NST * NFC
            while dn.m < 4 * NFC or events:
                dn.emit(step, 4)
                run_events(step)
                step += 1

    def phase_ret_proj(layer, hsrc):
        j = layer // 2
        with Phase(k) as P:
            wblk = load_w_blocks(P, "win", ret_w_in[j], 8, [(i_ * D, D) for i_ in range(6)])
            fr = Front(P, hsrc, g_mix_pre[layer], 7)
            qst = P.T("rqst", [128, 8, 512], BF16, dma=True)
            kst = P.T("rkst", [128, 8, 512], BF16, dma=True)
            vst = [P.T("rvst", [128, 2048], BF16, dma=True) for _ in range(2)]
            gst = [P.T("rgst", [128, 2048], BF16, dma=True) for _ in range(2)]
            cs_t, cs_b = P.T("cos", [128, 512], F32, dma=True)
            sn_t, sn_b = P.T("sin", [128, 512], F32, dma=True)
            tm = [P.T("rtm", [128, 512], F32) for _ in range(4)]
            fr.load(0)
            bc = 0
            ev = 0
            for st in range(8):
                cs = slice(st * 512, (st + 1) * 512)
                k.dma("sp", cs_t[:], c_cosT[:, cs], cs_b, writes=[cs_b])
                k.dma("sp", sn_t[:], c_sinT[:, cs], sn_b, writes=[sn_b])
                if st == 0:
                    cur = fr.emit(0)
                    fr.load(1)
                hnT, hnTb, _, _ = cur
                for which, ((s_t, s_b), dst) in enumerate(((qst, qT_d), (kst, kT_d))):
                    for h in range(4):
                        bks = []
                        for dc in range(2):
                            bk = bc % 6
                            bc += 1
                            bks.append(bk)
                            w, wb = wblk[which]
                            col = h * 256 + dc * 128
                            for c in range(8):
                                k.op("pe", lambda e: e.matmul(ps[:, bk, :], lhsT=w[:, c, col:col + 128],
                                                              rhs=hnT[:, c, :], start=(c == 0), stop=(c == 7)),
                                     [wb, hnTb], [PB[bk]])
                        x1, x2 = bks
                        for ti, (xb_, tab, tabb) in enumerate(((x1, cs_t, cs_b), (x2, sn_t, sn_b),
                                                               (x1, sn_t, sn_b), (x2, cs_t, cs_b))):
                            k.op("dve", lambda e: e.tensor_tensor(out=tm[ti][0][:], in0=ps[:, xb_, :], in1=tab[:],
                                                                  op=ALU.mult), [PB[xb_], tabb], [tm[ti][1]])
                        k.op("pool", lambda e: e.tensor_tensor(out=s_t[:, h * 2, :], in0=tm[0][0][:], in1=tm[1][0][:],
                                                               op=ALU.subtract), [tm[0][1], tm[1][1]], [s_b])
                        k.op("pool", lambda e: e.tensor_tensor(out=s_t[:, h * 2 + 1, :], in0=tm[2][0][:],
                                                               in1=tm[3][0][:], op=ALU.add),
                             [tm[2][1], tm[3][1]], [s_b])
                    k.dma("sp", dst.rearrange("h p t -> p h t")[:, :, cs], s_t[:], s_b, reads=[s_b])
                    if st + 1 < 8 and which == 0:
                        fr.emit_norm(st + 1)
                if st + 1 < 8:
                    cur = fr.emit_tr(st + 1)
                    if st + 2 < 8:
                        fr.load(st + 2)
                for t in range(4):
                    row = st * 4 + t
                    for which, (stg, dst) in enumerate(((vst, v_d), (gst, g_d))):
                        s_t, s_b = stg[row % 2]
                        for h in range(4):
                            bk = bc % 6
                            bc += 1
                            w, wb = wblk[2 + which * 2 + h // 2]
                            col = (h % 2) * 512
                            for c in range(8):
                                k.op("pe", lambda e: e.matmul(ps[:, bk, :], lhsT=hnT[:, c, t * 128:(t + 1) * 128],
                                                              rhs=w[:, c, col:col + 512],
                                                              start=(c == 0), stop=(c == 7)), [wb, hnTb], [PB[bk]])
                            if which == 1:
                                k.op("act", lambda e: e.activation(out=s_t[:, h * 512:(h + 1) * 512],
                                                                   in_=ps[:, bk, :], func=AF.Silu), [PB[bk]], [s_b])
                            else:
                                k.op("dve", lambda e: e.tensor_copy(out=s_t[:, h * 512:(h + 1) * 512],
                                                                    in_=ps[:, bk, :]), [PB[bk]], [s_b])
                        k.dma("sp", dst[row * 128:(row + 1) * 128, :], s_t[:], s_b, reads=[s_b])

    def phase_ret_rec(layer):
        j = layer // 2
        with Phase(k) as P:
            dpt, dptb = P.T("dpt", [128, 512], F32, dma=True)
            rc, rcb = P.T("rcol", [128, 16], F32, dma=True)
            gn, gnb = P.T("gncol", [128, 16], F32, dma=True)
            k.dma("sp", dpt[:], c_dpt[:, :], dptb, writes=[dptb])
            k.dma("sp", rc[:], c_rcol[:, :], rcb, writes=[rcb])
            k.dma("sp", gn[:], ret_gn[j], gnb, writes=[gnb])
            gnx, gnxb = P.T("gnx", [128, 16, 128], F32)
            k.op("pool", lambda e: e.memset(gnx[:], 1.0), [], [gnxb])
            for q_ in range(16):
                k.op("pool", lambda e: e.tensor_scalar_mul(out=gnx[:, q_, :], in0=gnx[:, q_, :],
                                                           scalar1=gn[:, q_:q_ + 1]), [gnb, gnxb], [gnxb])
            qg = [P.T("qg", [128, 8, 512], BF16, dma=True) for _ in range(2)]
            kg = [P.T("kg", [128, 8, 512], BF16, dma=True) for _ in range(2)]
            vg = [P.T("vg", [128, 4, 2048], BF16, dma=True) for _ in range(2)]
            gg = [P.T("gg", [128, 4, 2048], BF16, dma=True) for _ in range(2)]
            yT = [P.T("ryT", [128, 16, 512], BF16, dma=True) for _ in range(2)]
            St = [P.T("St", [128, 2, 512], F32) for _ in range(4)]
            Sb = [P.T("Sb", [128, 2, 512], BF16) for _ in range(4)]
            PT = [P.T("PT", [128, 128], BF16) for _ in range(4)]
            kt = [P.T("ktok", [128, 256], BF16) for _ in range(4)]
            NY = 4
            y1 = [P.T("y1", [128, 512], F32) for _ in range(NY)]
            yk = [P.T("ytok", [128, 512], BF16) for _ in range(NY)]
            st6 = [P.T("st6", [128, 1, 6], F32) for _ in range(NY)]
            mv = [P.T("mv", [128, 8], F32) for _ in range(NY)]

            def load(g):
                cs = slice(g * 512, (g + 1) * 512)
                for (t_, b_), src in ((qg[g % 2], qT_d), (kg[g % 2], kT_d)):
                    k.dma("sp", t_[:], src.rearrange("h p t -> p h t")[:, :, cs], b_, writes=[b_])
                for (t_, b_), src in ((vg[g % 2], v_d), (gg[g % 2], g_d)):
                    k.dma("sp", t_[:], src.rearrange("(t p) n -> p t n", p=128)[:, g * 4:(g + 1) * 4, :], b_,
                          writes=[b_])

            load(0)
            oc = 0
            sc = 0
            pend = []
            kt7 = psbf(7)
            kt1 = psbf(1)

            def flush_one():
                if pend:
                    fn, after = pend.pop(0)
                    fn()
                    if after is not None:
                        after()

            for g in range(8):
                if g + 1 < 8:
                    load(g + 1)
                q_t, q_b = qg[g % 2]
                k_t, k_b = kg[g % 2]
                v_t, v_b = vg[g % 2]
                g_t, g_b = gg[g % 2]
                y_t, y_b = yT[g % 2]
                for t in range(4):
                    T_ = g * 4 + t
                    tc = slice(t * 128, (t + 1) * 128)
                    last = (T_ == NT - 1)
                    for h in range(4):
                        for c in range(2):
                            k.op("pe", lambda e: e.matmul(ps[:, 0, h * 128:(h + 1) * 128], lhsT=k_t[:, h * 2 + c, tc],
                                                          rhs=q_t[:, h * 2 + c, tc], start=(c == 0), stop=(c == 1)),
                                 [k_b, q_b], [PB[0]])
                    if not last:
                        for h in range(4):
                            for c in range(2):
                                k.op("pe", lambda e: e.transpose(out=kt1[:, (h * 2 + c) * 128:(h * 2 + c + 1) * 128],
                                                                 in_=k_t[:, h * 2 + c, tc], identity=ident[:]),
                                     [k_b, identB], [PB[1]])
                    for h in range(4):
                        k.op("dve", lambda e: e.tensor_tensor(out=PT[h][0][:], in0=ps[:, 0, h * 128:(h + 1) * 128],
                                                              in1=dpt[:, h * 128:(h + 1) * 128], op=ALU.mult),
                             [PB[0], dptb], [PT[h][1]])
                        if not last:
                            k.op("act", lambda e: e.activation(out=kt[h][0][:], in_=kt1[:, h * 256:(h + 1) * 256],
                                                               func=AF.Identity, scale=rc[:, 8 + h:9 + h]),
                                 [PB[1], rcb], [kt[h][1]])
                    for h in range(4):
                        ob = 2 + oc % 3
                        oc += 1
                        vs = v_t[:, t, h * 512:(h + 1) * 512]
                        k.op("pe", lambda e: e.matmul(ps[:, ob, :], lhsT=PT[h][0][:], rhs=vs, start=True,
                                                      stop=(T_ == 0)), [PT[h][1], v_b], [PB[ob]])
                        if T_ > 0:
                            for c in range(2):
                                k.op("pe", lambda e: e.matmul(ps[:, ob, :], lhsT=q_t[:, h * 2 + c, tc],
                                                              rhs=Sb[h][0][:, c, :], start=False, stop=(c == 1)),
                                     [q_b, Sb[h][1]], [PB[ob]])
                        if not last:
                            for c in range(2):
                                k.op("pe", lambda e: e.matmul(ps[:, 5 + c, :], lhsT=kt[h][0][:, c * 128:(c + 1) * 128],
                                                              rhs=vs, start=True, stop=True),
                                     [kt[h][1], v_b], [PB[5 + c]])
                        s6, s6b = st6[sc % NY]
                        m, mb = mv[sc % NY]
                        y1t, y1b = y1[sc % NY]
                        ykt, ykb = yk[sc % NY]
                        sc += 1
                        k.op("dve", lambda e: e.bn_stats(out=s6[:, 0, :], in_=ps[:, ob, :]), [PB[ob]], [s6b])
                        k.op("dve", lambda e: e.bn_aggr(out=m[:, 0:2], in_=s6[:]), [s6b], [mb])
                        k.op("pool", lambda e: e.tensor_scalar(out=m[:, 2:3], in0=m[:, 1:2], scalar1=rc[:, 4 + h:5 + h],
                                                               scalar2=GN_EPS, op0=ALU.mult, op1=ALU.add),
                             [mb, rcb], [mb])
                        k.op("pool", lambda e: e.tensor_tensor(out=m[:, 4:5], in0=m[:, 2:3], in1=nhalf[:],
                                                               op=ALU.pow), [mb, nhalfB], [mb])
                        k.op("pool", lambda e: e.tensor_tensor(out=m[:, 5:6], in0=m[:, 4:5], in1=rc[:, h:h + 1],
                                                               op=ALU.mult), [mb, rcb], [mb])
                        k.op("pool", lambda e: e.tensor_tensor(out=m[:, 6:7], in0=m[:, 0:1], in1=m[:, 5:6],
                                                               op=ALU.mult), [mb], [mb])
                        k.op("pool", lambda e: e.tensor_scalar_mul(out=m[:, 6:7], in0=m[:, 6:7], scalar1=-1.0),
                             [mb], [mb])
                        k.op("act", lambda e: e.activation(out=y1t[:], in_=ps[:, ob, :], func=AF.Identity,
                                                           scale=m[:, 5:6], bias=m[:, 6:7]), [PB[ob], mb], [y1b])
                        k.op("pool", lambda e: e.tensor_tensor(out=ykt[:], in0=y1t[:],
                                                               in1=g_t[:, t, h * 512:(h + 1) * 512], op=ALU.mult),
                             [y1b, g_b], [ykb])
                        if not last:
                            if T_ == 0:
                                k.op("dve", lambda e: e.tensor_copy(out=St[h][0][:], in_=ps[:, 5:7, :]),
                                     [PB[5], PB[6]], [St[h][1]])
                            else:
                                k.op("dve", lambda e: e.scalar_tensor_tensor(out=St[h][0][:], in0=St[h][0][:],
                                                                             scalar=_SDEC[h], in1=ps[:, 5:7, :],
                                                                             op0=ALU.mult, op1=ALU.add),
                                     [PB[5], PB[6], St[h][1]], [St[h][1]])
                            k.op("act", lambda e: e.copy(out=Sb[h][0][:], in_=St[h][0][:]),
                                 [St[h][1]], [Sb[h][1]])

                        def later(h=h, ykt=ykt, ykb=ykb, y_t=y_t, y_b=y_b, tc=tc):
                            for q in range(4):
                                k.op("pe", lambda e: e.transpose(out=kt7[:, q * 128:(q + 1) * 128],
                                                                 in_=ykt[:, q * 128:(q + 1) * 128], identity=ident[:]),
                                     [ykb, identB], [PB[7]])
                            k.op("dve", lambda e: e.tensor_tensor(
                                out=y_t[:, h * 4:(h + 1) * 4, tc],
                                in0=kt7[:, 0:512].rearrange("p (q n) -> p q n", q=4),
                                in1=gnx[:, h * 4:(h + 1) * 4, :], op=ALU.mult), [PB[7], gnxb], [y_b])

                        after = None
                        if t == 3 and h == 3:
                            after = (lambda g=g, y_t=y_t, y_b=y_b:
                                     k.dma("sp", oT_d[g], y_t[:], y_b, reads=[y_b]))
                        pend.append((later, after))
                        if len(pend) > 2:
                            flush_one()
                if g == 7:
                    while pend:
                        flush_one()

    hcur = x
    for layer in layers:
        last = layer == layers[-1]
        if layer % 2 == 0:
            if want():
                phase_sb_proj(layer, hcur)
            if want():
                phase_sb_attn()
            w3, kc3, wdn_pre = sb_w_o[layer // 2], 8, True
        else:
            if want():
                phase_ret_proj(layer, hcur)
            if want():
                phase_ret_rec(layer)
            w3, kc3, wdn_pre = ret_w_o[layer // 2], 16, False
        do3, do4 = want(), want()
        with Phase(k) as PW:
            pre, loaders = ffn_weight_loaders(PW, layer, wdn_pre) if (do4 and PREFETCH_FFN) else ({}, [])
            if do3:
                phase_outproj(layer, w3, kc3, hcur, hs, background=loaders)
            else:
                for f_ in loaders:
                    f_()
            hcur = hs
            if do4:
                phase_ffn(layer, hs, y if last else hs, pre=pre)
    k.barrier([identB])
    k.nc_ref = nc
    return nc, k


def _host_inputs(inputs):
    c, _ = _consts()
    f = lambda a: np.ascontiguousarray(np.asarray(a, dtype=np.float32))
    shared = {
        "norm_mix_pre": f(inputs["norm_mix_pre"]), "norm_mix_post": f(inputs["norm_mix_post"]),
        "norm_ffn_pre": f(inputs["norm_ffn_pre"]), "norm_ffn_post": f(inputs["norm_ffn_post"]),
        "sb_w_qkv": f(inputs["sb_w_qkv"]), "sb_w_o": f(inputs["sb_w_o"]),
        "ret_w_in": f(inputs["ret_w_in"]), "ret_w_o": f(inputs["ret_w_o"]),
        "ret_gn": np.ascontiguousarray(f(inputs["ret_gn"]).reshape(2, 16, 128).transpose(0, 2, 1)),
        "ffn_w_up": f(inputs["ffn_w_up"]), "ffn_w_down": f(inputs["ffn_w_down"]),
        "ffn_cw": np.ascontiguousarray(f(inputs["ffn_conv_w"]).reshape(DEPTH, 3, 44, 128).transpose(0, 3, 1, 2)),
        "ffn_cb": np.ascontiguousarray(f(inputs["ffn_conv_b"]).reshape(DEPTH, 44, 128).transpose(0, 2, 1)),
        "c_ident": c["ident"], "c_negtri": c["negtri"], "c_selk": c["selk"], "c_zero": c["zero"], "c_esel": c["esel"],
        "c_masks": c["masks"], "c_cosT": c["cosT"], "c_sinT": c["sinT"], "c_dpt": c["dpt"], "c_rcol": c["rcol"],
    }
    return shared


def kernel(**inputs):
    xs = np.asarray(inputs["x"], dtype=np.float32)
    shared = _host_inputs(inputs)
    nc, _ = build()
    in_maps = [dict(shared, x=np.ascontiguousarray(xs[b])) for b in range(8)]
    res = run_bass_kernel_spmd(nc, in_maps, core_ids=list(range(8)))
    return np.stack([np.asarray(r["y"], dtype=np.float32) for r in res.results], axis=0)
```

```python
import math
from contextlib import ExitStack

import numpy as np
import ml_dtypes

import concourse.bass as bass
import concourse.mybir as mybir
from concourse.bass_utils import run_bass_kernel_spmd

F32 = mybir.dt.float32
BF16 = mybir.dt.bfloat16
AF = mybir.ActivationFunctionType
ALU = mybir.AluOpType

D = 1024
S = 4096
NT = S // 128
DEPTH = 4
DFF = 2816
NFC = DFF // 128
EPS = 1e-6
GN_EPS = 1e-5
LAG = 2
S2_LAG = 5
PREFETCH_FFN = False


class Buf:
    __slots__ = ("name", "w", "r", "sem", "semval", "slot")

    def __init__(self, name):
        self.name = name
        self.w = None
        self.r = {}
        self.sem = None
        self.semval = 0
        self.slot = None


class K:
    def __init__(self, nc, n_dma_sems=64):
        self.nc = nc
        self.stack = ExitStack()
        self.engs = {"pe": nc.tensor, "act": nc.scalar, "dve": nc.vector, "pool": nc.gpsimd, "sp": nc.sync}
        self.esem, self.ecnt = {}, {}
        self.waited = {e: {} for e in self.engs}
        for e in self.engs:
            self.esem[e] = self.stack.enter_context(nc.semaphore("es_" + e))
            self.ecnt[e] = 0
        self.sempool = [[self.stack.enter_context(nc.semaphore("ds%d" % i)), 0] for i in range(n_dma_sems)]
        self.free_slots = list(range(n_dma_sems))
        self.n_inst = 0
        self.uid = 0

    def buf(self, name, dma=False):
        b = Buf(name)
        if dma:
            b.slot = self.free_slots.pop()
            b.sem, b.semval = self.sempool[b.slot]
        return b

    def release(self, b):
        if b.slot is not None:
            self.sempool[b.slot][1] = b.semval
            self.free_slots.append(b.slot)
            b.slot = None

    def _wait(self, eng, ticket):
        if ticket is None:
            return
        sem, val = ticket
        if eng == "pe" and sem is self.esem["pe"]:
            return
        key = id(sem)
        if self.waited[eng].get(key, 0) >= val:
            return
        self.waited[eng][key] = val
        self.engs[eng].wait_ge(sem, val)
        self.n_inst += 1

    def op(self, eng, fn, reads=(), writes=()):
        for b in reads:
            self._wait(eng, b.w)
        for b in writes:
            self._wait(eng, b.w)
            for t in list(b.r.values()):
                self._wait(eng, t)
        inst = fn(self.engs[eng])
        self.ecnt[eng] += 1
        sem = self.esem[eng]
        inst.then_inc(sem, 1)
        self.n_inst += 1
        t = (sem, self.ecnt[eng])
        for b in reads:
            b.r[id(sem)] = t
        for b in writes:
            b.w = t
            b.r = {}
        return t

    def dma(self, q, out, in_, sb, reads=(), writes=(), **kw):
        for b in reads:
            self._wait(q, b.w)
        for b in writes:
            if not (b.w is not None and b is sb and b.w[0] is sb.sem):
                self._wait(q, b.w)
            for t in list(b.r.values()):
                self._wait(q, t)
        inst = self.engs[q].dma_start(out=out, in_=in_, **kw)
        sb.semval += 16
        inst.then_inc(sb.sem, 16)
        self.n_inst += 1
        t = (sb.sem, sb.semval)
        for b in reads:
            b.r[id(sb.sem)] = t
        for b in writes:
            b.w = t
            b.r = {}
        return t

    def barrier(self, bufs=()):
        tickets = [(self.esem[e], self.ecnt[e]) for e in self.engs if self.ecnt[e] > 0]
        tickets += [(b.sem, b.semval) for b in bufs if b.sem is not None and b.semval > 0]
        for e in self.engs:
            for t in tickets:
                if t[0] is self.esem[e]:
                    continue
                self._wait(e, t)


class Phase:
    def __init__(self, k):
        self.k = k
        self.nc = k.nc

    def __enter__(self):
        self.st = ExitStack()
        self.bufs = []
        return self

    def T(self, name, shape, dt, dma=False):
        self.k.uid += 1
        t = self.st.enter_context(self.nc.sbuf_tensor("%s_%d" % (name, self.k.uid), list(shape), dt))
        b = self.k.buf(name, dma)
        self.bufs.append(b)
        return t, b

    def __exit__(self, *a):
        self.k.barrier(self.bufs)
        for b in self.bufs:
            self.k.release(b)
        self.st.close()
        return False


def _consts():
    bf = ml_dtypes.bfloat16
    c = {}
    c["ident"] = np.eye(128, dtype=np.float32).astype(bf)
    j = np.arange(128)[:, None]
    s = np.arange(128)[None, :]
    c["negtri"] = (-(j >= s).astype(np.float32)).astype(bf)
    bp = np.arange(32)[:, None, None]
    b = np.arange(32)[None, :, None]
    selk = np.zeros((64, S), np.float32)
    selk[0:32] = -(np.arange(32)[:, None] > (np.arange(S)[None, :] // 128)).astype(np.float32)
    c["selk"] = selk.astype(bf)
    c["zero"] = np.zeros((64, S), np.float32).astype(bf)
    es = np.zeros((128, 32, 128), np.float32)
    for i in range(32):
        es[:, i, i] = 1.0
        es[:, i, 64 + i] = 1.0
    c["esel"] = es.reshape(128, 32 * 128).astype(bf)
    p = np.arange(128)[:, None]
    xx = np.arange(512)[None, :]
    c["masks"] = np.concatenate([((xx - 128 * kk - p) > 0).astype(np.float32) for kk in range(4)], 1).astype(bf)
    half = 128
    inv_freq = (np.float32(10000.0) ** (-np.arange(half, dtype=np.float32) / np.float32(half))).astype(np.float32)
    pos = np.arange(S, dtype=np.float32)
    ang = (pos[None, :] * inv_freq[:, None]).astype(np.float32)
    c["cosT"] = np.cos(ang.astype(np.float64)).astype(np.float32)
    c["sinT"] = np.sin(ang.astype(np.float64)).astype(np.float32)
    i = np.arange(128)
    dpt = np.zeros((128, 4, 128), np.float64)
    rc = np.zeros((128, 16), np.float64)
    sdec = []
    for h in range(4):
        lg = np.log1p(-np.exp2(-5.0 - h))
        a = np.exp(lg * (i + 1.0))
        ci, cj = i[:, None] // 64, i[None, :] // 64
        li, lj = i[:, None] % 64, i[None, :] % 64
        Dm = np.where(ci == cj, np.exp(lg * np.abs(li - lj)),
                      np.where(ci > cj, np.exp(lg * (i[:, None] - i[None, :]).clip(0)), 0.0))
        dpt[:, h, :] = (Dm / a[:, None]).T * (256.0 ** -0.5)
        rc[:, h] = a
        rc[:, 4 + h] = a * a
        rc[:, 8 + h] = np.exp(lg * (127.0 - i)) * (256.0 ** -0.5)
        sdec.append(float(np.exp(lg * 128.0)))
    c["dpt"] = dpt.reshape(128, 512).astype(np.float32)
    c["rcol"] = rc.astype(np.float32)
    return c, sdec


_SDEC = [float(np.exp(np.log1p(-np.exp2(-5.0 - h)) * 128.0)) for h in range(4)]


def build(n_sub=None, dbg=False, layers=(0, 1, 2, 3)):
    nc = bass.Bass("TRN2", target_bir_lowering=False)

    def din(name, shape, dt=F32):
        return nc.dram_tensor(name, list(shape), dt, kind="ExternalInput").ap()

    def dscr(name, shape, dt):
        if dbg:
            return nc.dram_tensor(name, list(shape), dt, kind="ExternalOutput").ap()
        return nc.dram_tensor(name, list(shape), dt).ap()

    x = din("x", [S, D])
    g_mix_pre = din("norm_mix_pre", [DEPTH, D])
    g_mix_post = din("norm_mix_post", [DEPTH, D])
    g_ffn_pre = din("norm_ffn_pre", [DEPTH, D])
    g_ffn_post = din("norm_ffn_post", [DEPTH, D])
    sb_w_qkv = din("sb_w_qkv", [2, D, 3072])
    sb_w_o = din("sb_w_o", [2, D, D])
    ret_w_in = din("ret_w_in", [2, D, 6144])
    ret_gn = din("ret_gn", [2, 128, 16])
    ret_w_o = din("ret_w_o", [2, 2048, D])
    ffn_w_up = din("ffn_w_up", [DEPTH, D, 2 * DFF])
    ffn_cw = din("ffn_cw", [DEPTH, 128, 3, 44])
    ffn_cb = din("ffn_cb", [DEPTH, 128, 44])
    ffn_w_down = din("ffn_w_down", [DEPTH, DFF, D])
    c_ident = din("c_ident", [128, 128], BF16)
    c_negtri = din("c_negtri", [128, 128], BF16)
    c_selk = din("c_selk", [64, S], BF16)
    c_zero = din("c_zero", [64, S], BF16)
    c_esel = din("c_esel", [128, 32 * 128], BF16)
    c_masks = din("c_masks", [128, 4 * 512], BF16)
    c_cosT = din("c_cosT", [128, S])
    c_sinT = din("c_sinT", [128, S])
    c_dpt = din("c_dpt", [128, 512])
    c_rcol = din("c_rcol", [128, 16])
    y = nc.dram_tensor("y", [S, D], F32, kind="ExternalOutput").ap()

    hs = dscr("hs", [S, D], F32)
    qT_d = dscr("qT_d", [8, 128, S], BF16)
    kT_d = dscr("kT_d", [8, 128, S], BF16)
    v_d = dscr("v_d", [S, 2048], BF16)
    g_d = dscr("g_d", [S, 2048], BF16)
    oT_d = dscr("oT_d", [8, 128, 16, 512], BF16)

    k = K(nc)
    gs = k.stack
    ps = gs.enter_context(nc.psum_tensor("ps", [128, 8, 512], F32))
    PB = [k.buf("psb%d" % i) for i in range(8)]
    ident = gs.enter_context(nc.sbuf_tensor("ident", [128, 128], BF16))
    identB = k.buf("ident", dma=True)
    k.dma("sp", ident[:], c_ident[:, :], identB, writes=[identB])

    nhalf = gs.enter_context(nc.sbuf_tensor("nhalf", [128, 1], F32))
    nhalfB = k.buf("nhalf")
    k.op("pool", lambda e: e.memset(nhalf[:], -0.5), [], [nhalfB])

    def rstd_pool(out_ap, in_ap, buf, scale):
        k.op("pool", lambda e: e.tensor_scalar(out=out_ap, in0=in_ap, scalar1=scale, scalar2=EPS,
                                               op0=ALU.mult, op1=ALU.add), [buf], [buf])
        k.op("pool", lambda e: e.tensor_tensor(out=out_ap, in0=out_ap, in1=nhalf[:], op=ALU.pow),
             [buf, nhalfB], [buf])

    def psbf(bank):
        return ps[:, bank, :].bitcast(BF16)

    sub = [0]

    def want():
        sub[0] += 1
        return n_sub is None or sub[0] <= n_sub

    def load_w(P, name, src, nchunk, ncols, col0=0):
        t, b = P.T(name, [128, nchunk, ncols], BF16, dma=True)
        v = src.rearrange("(c p) n -> p c n", p=128)
        for c in range(nchunk):
            k.dma("pool", t[:, c, :], v[:, c, col0:col0 + ncols], b, writes=[b], max_dma_last_dim=4096)
        return t, b

    def load_w_blocks(P, name, src, nchunk, blocks):
        out = []
        v = src.rearrange("(c p) n -> p c n", p=128)
        for (col0, ncols) in blocks:
            t, b = P.T(name, [128, nchunk, ncols], BF16, dma=True)
            k.dma("pool", t[:], v[:, :, col0:col0 + ncols], b, writes=[b], max_dma_last_dim=4096)
            out.append((t, b))
        return out

    def load_bcast(P, name, vec):
        t, b = P.T(name, [128, D], F32, dma=True)
        k.dma("sp", t[:], vec.partition_broadcast(128), b, writes=[b])
        return t, b

    class Front:
        def __init__(self, P, hsrc, gvec, trbank, hb_n=1, nt=4, halo=0, use_pool=False):
            self.P, self.hsrc, self.trbank, self.nt, self.halo = P, hsrc, trbank, nt, halo
            self.use_pool = use_pool
            self.hb = [P.T("hb", [128, nt, D], F32, dma=True) for _ in range(hb_n)]
            self.hnT = [P.T("hnT", [128, 8, halo + nt * 128], BF16) for _ in range(2)]
            self.hnb = [P.T("hnb", [128, D], BF16) for _ in range(nt)]
            self.junk = P.T("junk", [128, D], BF16)
            self.stat = [P.T("stat", [128, 4], F32) for _ in range(4)]
            self.g = load_bcast(P, "gpre", gvec)

        def load(self, st):
            t, b = self.hb[st % len(self.hb)]
            src = self.hsrc.rearrange("(t p) n -> p t n", p=128)[:, st * self.nt:(st + 1) * self.nt, :]
            k.dma("sp", t[:], src, b, writes=[b])

        def norm_a(self, st, t):
            hb_t, hb_b = self.hb[st % len(self.hb)]
            jk_t, jk_b = self.junk
            sm, smb = self.stat[t]
            k.op("act", lambda e: e.activation(out=jk_t[:], in_=hb_t[:, t, :], func=AF.Square,
                                               accum_out=sm[:, 0:1]), [hb_b], [jk_b, smb])

        def norm_b(self, st, t):
            sm, smb = self.stat[t]
            if self.use_pool:
                rstd_pool(sm[:, 2:3], sm[:, 0:1], smb, 1.0 / D)
            else:
                k.op("act", lambda e: e.activation(out=sm[:, 1:2], in_=sm[:, 0:1], func=AF.Sqrt,
                                                   scale=1.0 / D, bias=EPS), [smb], [smb])
                k.op("dve", lambda e: e.reciprocal(out=sm[:, 2:3], in_=sm[:, 1:2]), [smb], [smb])

        def norm_c(self, st, t):
            hb_t, hb_b = self.hb[st % len(self.hb)]
            g_t, g_b = self.g
            sm, smb = self.stat[t]
            hn, hnb_ = self.hnb[t]
            k.op("dve", lambda e: e.scalar_tensor_tensor(out=hn[:], in0=hb_t[:, t, :], scalar=sm[:, 2:3],
                                                         in1=g_t[:], op0=ALU.mult, op1=ALU.mult),
                 [hb_b, smb, g_b], [hnb_])

        def emit_norm(self, st, tiles=None):
            for t in (range(self.nt) if tiles is None else tiles):
                self.norm_a(st, t)
                self.norm_b(st, t)
                self.norm_c(st, t)

        def tr_begin(self, st):
            hnT_t, hnT_b = self.hnT[st % 2]
            H = self.halo
            if H:
                if st == 0:
                    k.op("dve", lambda e: e.memset(hnT_t[:, :, 0:H], 0.0), [], [hnT_b])
                else:
                    p_t, p_b = self.hnT[(st - 1) % 2]
                    k.op("dve", lambda e: e.tensor_copy(out=hnT_t[:, :, 0:H],
                                                         in_=p_t[:, :, self.nt * 128:self.nt * 128 + H]),
                         [p_b], [hnT_b])

        def tr_tile(self, st, t):
            hb_t, hb_b = self.hb[st % len(self.hb)]
            hnT_t, hnT_b = self.hnT[st % 2]
            trp = psbf(self.trbank)
            H = self.halo
            hn, hnb_ = self.hnb[t]
            for c in range(8):
                k.op("pe", lambda e: e.transpose(out=trp[:, c * 128:(c + 1) * 128],
                                                 in_=hn[:, c * 128:(c + 1) * 128], identity=ident[:]),
                     [hnb_, identB], [PB[self.trbank]])
            k.op("dve", lambda e: e.tensor_copy(out=hnT_t[:, :, H + t * 128:H + (t + 1) * 128],
                                                in_=trp.rearrange("p (c n) -> p c n", c=8)),
                 [PB[self.trbank]], [hnT_b])
            return hnT_t, hnT_b, hb_t, hb_b

        def emit_tr(self, st):
            self.tr_begin(st)
            for t in range(self.nt):
                r = self.tr_tile(st, t)
            return r

        def emit(self, st):
            self.emit_norm(st)
            return self.emit_tr(st)

    class Post:
        def __init__(self, P, gvec, ntmp=2, split_add=False, use_pool=False):
            self.split_add = split_add
            self.use_pool = use_pool
            self.g = load_bcast(P, "gpost", gvec)
            self.junk = P.T("pjunk", [128, 512], BF16)
            self.tmp = [P.T("ptmp", [128, D], F32) for _ in range(ntmp)]
            self.stat = [P.T("pstat", [128, 8], F32) for _ in range(2)]
            self.n = 0

        def stage_a(self, banks):
            jk, jkb = self.junk
            tmp, tmpb = self.tmp[self.n % len(self.tmp)]
            sm, smb = self.stat[self.n % 2]
            self.n += 1
            for hf in range(2):
                k.op("act", lambda e: e.activation(out=jk[:], in_=ps[:, banks[hf], :], func=AF.Square,
                                                   accum_out=sm[:, hf:hf + 1]), [PB[banks[hf]]], [jkb, smb])
            return {"banks": banks, "tmp": tmp, "tmpb": tmpb, "sm": sm, "smb": smb}

        def stage_b(self, c):
            sm, smb = c["sm"], c["smb"]
            if self.use_pool:
                k.op("pool", lambda e: e.tensor_tensor(out=sm[:, 2:3], in0=sm[:, 0:1], in1=sm[:, 1:2], op=ALU.add),
                     [smb], [smb])
                rstd_pool(sm[:, 4:5], sm[:, 2:3], smb, 1.0 / D)
            else:
                k.op("dve", lambda e: e.tensor_tensor(out=sm[:, 2:3], in0=sm[:, 0:1], in1=sm[:, 1:2], op=ALU.add),
                     [smb], [smb])
                k.op("act", lambda e: e.activation(out=sm[:, 3:4], in_=sm[:, 2:3], func=AF.Sqrt,
                                                   scale=1.0 / D, bias=EPS), [smb], [smb])
                k.op("dve", lambda e: e.reciprocal(out=sm[:, 4:5], in_=sm[:, 3:4]), [smb], [smb])

        def stage_c(self, c, h_ap, h_b):
            g_t, g_b = self.g
            banks, tmp, tmpb, sm, smb = c["banks"], c["tmp"], c["tmpb"], c["sm"], c["smb"]
            for hf in range(2):
                k.op("dve", lambda e: e.scalar_tensor_tensor(out=tmp[:, hf * 512:(hf + 1) * 512],
                                                             in0=ps[:, banks[hf], :], scalar=sm[:, 4:5],
                                                             in1=g_t[:, hf * 512:(hf + 1) * 512],
                                                             op0=ALU.mult, op1=ALU.mult),
                     [PB[banks[hf]], smb, g_b], [tmpb])
            if self.split_add:
                k.op("pool", lambda e: e.tensor_tensor(out=h_ap[:, 0:512], in0=h_ap[:, 0:512], in1=tmp[:, 0:512],
                                                       op=ALU.add), [tmpb, h_b], [h_b])
                k.op("dve", lambda e: e.tensor_tensor(out=h_ap[:, 512:D], in0=h_ap[:, 512:D], in1=tmp[:, 512:D],
                                                      op=ALU.add), [tmpb, h_b], [h_b])
            else:
                k.op("pool", lambda e: e.tensor_tensor(out=h_ap, in0=h_ap, in1=tmp[:], op=ALU.add),
                     [tmpb, h_b], [h_b])

        def emit(self, banks, h_ap, h_b):
            c = self.stage_a(banks)
            self.stage_b(c)
            self.stage_c(c, h_ap, h_b)

    def phase_sb_proj(layer, hsrc):
        j = layer // 2
        with Phase(k) as P:
            wblk = load_w_blocks(P, "wqkv", sb_w_qkv[j], 8, [(0, D), (D, D), (2 * D, D)])
            fr = Front(P, hsrc, g_mix_pre[layer], 7)
            qst = [P.T("qst", [128, 8, 512], BF16, dma=True) for _ in range(2)]
            kst = [P.T("kst", [128, 8, 512], BF16, dma=True) for _ in range(2)]
            vst = [P.T("vst", [128, 4, D], BF16, dma=True) for _ in range(2)]
            fr.load(0)
            bank = [0]

            def nb():
                bank[0] = (bank[0] + 1) % 6
                return bank[0]

            ev = [0]
            cur = fr.emit(0)
            fr.load(1)
            for st in range(8):
                hnT, hnTb, _, _ = cur
                cs = slice(st * 512, (st + 1) * 512)
                for which, (stg, dst, scale) in enumerate(((qst, qT_d, 0.125), (kst, kT_d, 1.0))):
                    s_t, s_b = stg[st % 2]
                    w, wb = wblk[which]
                    for hp in range(8):
                        bk = nb()
                        for c in range(8):
                            k.op("pe", lambda e: e.matmul(ps[:, bk, :],
                                                          lhsT=w[:, c, hp * 128:(hp + 1) * 128],
                                                          rhs=hnT[:, c, :], start=(c == 0), stop=(c == 7)),
                                 [wb, hnTb], [PB[bk]])
                        ev[0] += 1
                        if ev[0] % 2:
                            k.op("act", lambda e: e.mul(out=s_t[:, hp, :], in_=ps[:, bk, :], mul=scale),
                                 [PB[bk]], [s_b])
                        else:
                            k.op("dve", lambda e: e.tensor_scalar_mul(out=s_t[:, hp, :], in0=ps[:, bk, :],
                                                                      scalar1=scale), [PB[bk]], [s_b])
                    k.dma("sp", dst.rearrange("h p t -> p h t")[:, :, cs], s_t[:], s_b, reads=[s_b])
                    if st + 1 < 8 and which == 0:
                        fr.emit_norm(st + 1)
                if st + 1 < 8:
                    cur = fr.emit_tr(st + 1)
                    if st + 2 < 8:
                        fr.load(st + 2)
                v_t, v_b = vst[st % 2]
                w, wb = wblk[2]
                for t in range(4):
                    for hf in range(2):
                        bk = nb()
                        for c in range(8):
                            k.op("pe", lambda e: e.matmul(ps[:, bk, :], lhsT=hnT[:, c, t * 128:(t + 1) * 128],
                                                          rhs=w[:, c, hf * 512:(hf + 1) * 512],
                                                          start=(c == 0), stop=(c == 7)), [wb, hnTb], [PB[bk]])
                        ev[0] += 1
                        eng = "act" if ev[0] % 2 else "dve"
                        if eng == "act":
                            k.op("act", lambda e: e.copy(out=v_t[:, t, hf * 512:(hf + 1) * 512], in_=ps[:, bk, :]),
                                 [PB[bk]], [v_b])
                        else:
                            k.op("dve", lambda e: e.tensor_copy(out=v_t[:, t, hf * 512:(hf + 1) * 512],
                                                                in_=ps[:, bk, :]), [PB[bk]], [v_b])
                k.dma("sp", v_d.rearrange("(t p) n -> p t n", p=128)[:, st * 4:(st + 1) * 4, 0:D], v_t[:], v_b,
                      reads=[v_b])

    def phase_sb_attn():
        with Phase(k) as P:
            tri, trib = P.T("tri", [128, 128], BF16, dma=True)
            esel, eselb = P.T("esel", [128, 32 * 128], BF16, dma=True)
            msk, mskb = P.T("msk", [128, 4 * 512], BF16, dma=True)
            k.dma("sp", tri[:], c_negtri[:, :], trib, writes=[trib])
            k.dma("sp", esel[:], c_esel[:, :], eselb, writes=[eselb])
            k.dma("sp", msk[:], c_masks[:, :], mskb, writes=[mskb])
            qp = [[P.T("qp", [128, S], BF16, dma=True) for _e in range(2)] for _ in range(2)]
            kp = [[P.T("kp", [128, S], BF16, dma=True) for _e in range(2)] for _ in range(2)]
            vp = [P.T("vp", [128, NT, 128], BF16, dma=True) for _ in range(2)]
            oT = [P.T("oT", [128, S], BF16, dma=True) for _ in range(2)]
            Ls = [P.T("Ls%d" % i, [128, 32, 512], BF16)[0] for i in range(2)]
            LB = [[k.buf("L%d_%d" % (i, b_)) for b_ in range(32)] for i in range(2)]
            Asb = [P.T("Asb", [128, 512], BF16) for _ in range(4)]
            oh = [slice(64, 128), slice(0, 64)]
            hh = [slice(0, 64), slice(64, 128)]
            for i_ in range(2):
                for e_ in range(2):
                    k.dma("sp", kp[i_][e_][0][oh[e_], :], c_selk[:, :], kp[i_][e_][1], writes=[kp[i_][e_][1]])

            def load(hp):
                for e_ in range(2):
                    q_t, q_b = qp[hp % 2][e_]
                    k_t, k_b = kp[hp % 2][e_]
                    k.dma("sp", q_t[hh[e_], :], qT_d[hp, hh[e_], :], q_b, writes=[q_b])
                    k.dma("sp", q_t[oh[e_], :], c_zero[:, :], q_b, writes=[q_b])
                    k.dma("sp", k_t[hh[e_], :], kT_d[hp, hh[e_], :], k_b, writes=[k_b])
                k.dma("sp", vp[hp % 2][0][:],
                      v_d.rearrange("(t p) n -> p t n", p=128)[:, :, hp * 128:(hp + 1) * 128],
                      vp[hp % 2][1], writes=[vp[hp % 2][1]])

            load(0)
            cnt = {"w": 0, "a": 0}
            items = []

            def pass1(hp, J, e_, off):
                nb = 4 * J + 4
                cs = slice(J * 512, (J + 1) * 512)
                q_t, q_b = qp[hp % 2][e_]
                k_t, k_b = kp[hp % 2][e_]
                L, Lb = Ls[e_], LB[e_]
                for i in range(nb):
                    st_ = {}

                    def produce(i=i, st_=st_):
                        zb = cnt["w"] % 6
                        cnt["w"] += 1
                        st_["b"] = zb
                        c0 = 128 * max(0, i - 4 * J) if hp > 0 else 0
                        k.op("pe", lambda e: e.matmul(ps[:, zb, c0:512], lhsT=k_t[:, i * 128:(i + 1) * 128],
                                                      rhs=q_t[:, J * 512 + c0:(J + 1) * 512], start=True, stop=True),
                             [k_b, q_b], [PB[zb]])

                    def consume(jb=i, st_=st_):
                        zb = st_["b"]
                        c0 = 128 * max(0, jb - 4 * J) if hp > 0 else 0
                        k.op("act", lambda e: e.activation(out=L[:, off + jb, c0:512], in_=ps[:, zb, c0:512],
                                                           func=AF.Softplus), [PB[zb]], [Lb[off + jb]])
                        kk = jb - 4 * J
                        if kk >= 0:
                            k.op("dve", lambda e: e.tensor_tensor(out=L[:, off + jb, :], in0=L[:, off + jb, :],
                                                                  in1=msk[:, kk * 512:(kk + 1) * 512], op=ALU.mult),
                                 [mskb, Lb[off + jb]], [Lb[off + jb]])
                        k.op("pe", lambda e: e.matmul(ps[:, 6, :], lhsT=esel[:, jb * 128:(jb + 1) * 128],
                                                      rhs=L[:, off + jb, :], start=(jb == 0), stop=(jb == nb - 1)),
                             [eselb, Lb[off + jb]], [PB[6]])
                        if jb == nb - 1:
                            so = 64 if e_ == 0 else 0
                            k.op("dve", lambda e: e.tensor_copy(out=q_t[so:so + 32, cs], in_=ps[so:so + 32, 6, :]),
                                 [PB[6]], [q_b])

                    items.append((produce, consume))

            def pass2(hp, J, e_, off):
                nb = 4 * J + 4
                cs = slice(J * 512, (J + 1) * 512)
                q_t, q_b = qp[hp % 2][e_]
                k_t, k_b = kp[hp % 2][e_]
                v_t, v_b = vp[hp % 2]
                o_t, o_b = oT[hp % 2]
                L, Lb = Ls[e_], LB[e_]
                for i in range(nb):
                    st_ = {}

                    def produce(i=i, st_=st_):
                        pb = cnt["w"] % 6
                        cnt["w"] += 1
                        st_["b"] = pb
                        c0 = 128 * max(0, i - 4 * J) if hp > 0 else 0
                        k.op("pe", lambda e: e.matmul(ps[:, pb, c0:512], lhsT=k_t[:, i * 128:(i + 1) * 128],
                                                      rhs=q_t[:, J * 512 + c0:(J + 1) * 512], start=True, stop=False),
                             [k_b, q_b], [PB[pb]])
                        k.op("pe", lambda e: e.matmul(ps[:, pb, c0:512], lhsT=tri[:], rhs=L[:, off + i, c0:512],
                                                      start=False, stop=True), [trib, Lb[off + i]], [PB[pb]])

                    def consume(jb=i, st_=st_):
                        pb = st_["b"]
                        a_t, a_b = Asb[cnt["a"] % 4]
                        cnt["a"] += 1
                        c0 = 128 * max(0, jb - 4 * J) if hp > 0 else 0
                        k.op("act", lambda e: e.activation(out=a_t[:, c0:512], in_=ps[:, pb, c0:512], func=AF.Exp),
                             [PB[pb]], [a_b])
                        kk = jb - 4 * J
                        if kk >= 0:
                            k.op("dve", lambda e: e.tensor_tensor(out=a_t[:], in0=a_t[:],
                                                                  in1=msk[:, kk * 512:(kk + 1) * 512], op=ALU.mult),
                                 [mskb, a_b], [a_b])
                        k.op("pe", lambda e: e.matmul(ps[:, 7, :], lhsT=v_t[:, jb, :], rhs=a_t[:],
                                                      start=(jb == 0), stop=(jb == nb - 1)), [v_b, a_b], [PB[7]])
                        if jb == nb - 1:
                            k.op("dve", lambda e: e.tensor_copy(out=o_t[hh[e_], cs], in_=ps[hh[e_], 7, :]),
                                 [PB[7]], [o_b])
                            if e_ == 1 and J == 7:
                                k.dma("sp", oT_d.rearrange("g p c t -> p g c t")[:, :, hp, :],
                                      o_t.rearrange("p (g t) -> p g t", g=8), o_b, reads=[o_b])
                                if hp + 2 < 8:
                                    load(hp + 2)

                    items.append((produce, consume))

            load(1)
            jsets = [((0, 0), (6, 4)), ((1, 0), (5, 8)), ((2, 0), (4, 12)), ((3, 0),), ((7, 0),)]
            for hp in range(8):
                for js in jsets:
                    for (J, off) in js:
                        pass1(hp, J, 0, off)
                        pass1(hp, J, 1, off)
                    for (J, off) in js:
                        pass2(hp, J, 0, off)
                        pass2(hp, J, 1, off)
            for n in range(len(items) + S2_LAG):
                if n < len(items):
                    items[n][0]()
                if n - S2_LAG >= 0:
                    items[n - S2_LAG][1]()

    def phase_outproj(layer, w_src, kc, hsrc, hdst, background=()):
        background = list(background)
        with Phase(k) as P:
            w, wb = load_w(P, "wo", w_src, kc, D)
            post = Post(P, g_mix_post[layer], split_add=True)
            yT = [P.T("yT", [128, kc, 512], BF16, dma=True) for _ in range(2)]
            hb = [P.T("hbo", [128, 4, D], F32, dma=True) for _ in range(2)]

            def load(g):
                cs = slice(g * 512, (g + 1) * 512)
                k.dma("sp", yT[g % 2][0][:], oT_d[g, :, 0:kc, :], yT[g % 2][1], writes=[yT[g % 2][1]])
                k.dma("sp", hb[g % 2][0][:], hsrc.rearrange("(t p) n -> p t n", p=128)[:, g * 4:(g + 1) * 4, :],
                      hb[g % 2][1], writes=[hb[g % 2][1]])

            load(0)
            bc = 0
            for g in range(8):
                if g + 1 < 8:
                    load(g + 1)
                y_t, y_b = yT[g % 2]
                h_t, h_b = hb[g % 2]
                for t in range(4):
                    banks = []
                    for hf in range(2):
                        bk = bc % 8
                        bc += 1
                        banks.append(bk)
                        for c in range(kc):
                            k.op("pe", lambda e: e.matmul(ps[:, bk, :], lhsT=y_t[:, c, t * 128:(t + 1) * 128],
                                                          rhs=w[:, c, hf * 512:(hf + 1) * 512],
                                                          start=(c == 0), stop=(c == kc - 1)), [wb, y_b], [PB[bk]])
                    post.emit(banks, h_t[:, t, :], h_b)
                    if background:
                        background.pop(0)()
                k.dma("sp", hdst.rearrange("(t p) n -> p t n", p=128)[:, g * 4:(g + 1) * 4, :], h_t[:], h_b,
                      reads=[h_b])
            while background:
                background.pop(0)()

    def ffn_weight_loaders(PW, layer, with_down):
        wu, wub = PW.T("wup", [128, 8, 2 * DFF], BF16, dma=True)
        vu = ffn_w_up[layer].rearrange("(c p) n -> p c n", p=128)
        loaders = [(lambda c=c: k.dma("pool", wu[:, c, :], vu[:, c, :], wub, writes=[wub], max_dma_last_dim=4096))
                   for c in range(8)]
        pre = {"wu": (wu, wub)}
        if with_down:
            wd, wdb = PW.T("wdn", [128, NFC, D], BF16, dma=True)
            vd = ffn_w_down[layer].rearrange("(c p) n -> p c n", p=128)
            loaders += [(lambda c=c: k.dma("pool", wd[:, c, :], vd[:, c, :], wdb, writes=[wdb],
                                           max_dma_last_dim=4096)) for c in range(NFC)]
            pre["wd"] = (wd, wdb)
        return pre, loaders

    def phase_ffn(layer, hsrc, hdst, pre=None):
        TS, NTS, NST = 256, 2, 16
        pre = pre or {}
        with Phase(k) as P:
            ublocks = []
            for i_ in range(6):
                wcols = min(512, DFF - i_ * 512)
                ublocks += [(i_ * 512, wcols), (DFF + i_ * 512, wcols)]
            wub_l = load_w_blocks(P, "wup", ffn_w_up[layer], 8, ublocks)
            wd, wdb = pre["wd"] if "wd" in pre else load_w(P, "wdn", ffn_w_down[layer], NFC, D)
            fr = Front(P, hsrc, g_ffn_pre[layer], 7, hb_n=1, nt=NTS, halo=2, use_pool=True)
            post = Post(P, g_ffn_post[layer], ntmp=1, use_pool=True)
            cw, cwb = P.T("cw", [128, 3, 44], F32, dma=True)
            cb, cbb = P.T("cb", [128, 44], F32, dma=True)
            k.dma("sp", cw[:], ffn_cw[layer], cwb, writes=[cwb])
            k.dma("sp", cb[:], ffn_cb[layer], cbb, writes=[cbb])
            cv = [P.T("cv", [128, TS], F32) for _ in range(6)]
            gl = [P.T("gl", [128, TS], F32) for _ in range(3)]
            actTs = [P.T("actT", [128, NFC, TS], BF16) for _ in range(2)]
            hr = [P.T("hr", [128, D], F32, dma=True) for _ in range(2)]
            hsrc_t = hsrc.rearrange("(t p) n -> p t n", p=128)
            hdst_t = hdst.rearrange("(t p) n -> p t n", p=128)
            fr.load(0)
            bc = [0]
            uc = 0
            hrc = [0]
            events = {}

            def at(step, fn):
                events.setdefault(step, []).append(fn)

            def run_events(step):
                for fn in events.pop(step, []):
                    fn()

            class Down:
                def __init__(self, st_, actT, actTb):
                    self.st, self.actT, self.actTb, self.m, self.banks = st_, actT, actTb, 0, {}

                def emit(self, step, n):
                    for _ in range(n):
                        if self.m >= 4 * NFC:
                            return
                        gi, jc = self.m // NFC, self.m % NFC
                        t, hf = gi // 2, gi % 2
                        if jc == 0:
                            self.banks[gi] = 4 + bc[0] % 3
                            bc[0] += 1
                        bk = self.banks[gi]
                        actT, actTb = self.actT, self.actTb
                        k.op("pe", lambda e: e.matmul(ps[:, bk, :], lhsT=actT[:, jc, t * 128:(t + 1) * 128],
                                                      rhs=wd[:, jc, hf * 512:(hf + 1) * 512],
                                                      start=(jc == 0), stop=(jc == NFC - 1)),
                             [wdb, actTb], [PB[bk]])
                        self.m += 1
                        if jc == NFC - 1 and hf == 1:
                            self.schedule_post(step, t, (self.banks[gi - 1], self.banks[gi]))

                def schedule_post(self, step, t, banks):
                    tile = self.st * NTS + t
                    h_t, h_b = hr[hrc[0] % 2]
                    hrc[0] += 1
                    ctx = {}

                    def s_a():
                        k.dma("sp", h_t[:], hsrc_t[:, tile, :], h_b, writes=[h_b])
                        ctx["c"] = post.stage_a(banks)

                    def s_b():
                        post.stage_b(ctx["c"])

                    def s_c():
                        post.stage_c(ctx["c"], h_t[:], h_b)
                        k.dma("sp", hdst_t[:, tile, :], h_t[:], h_b, reads=[h_b])

                    at(step + 1, s_a)
                    at(step + 3, s_b)
                    at(step + 4, s_c)

            dn = None
            cur = fr.emit(0)
            for st in range(NST):
                hnT, hnTb, _, _ = cur
                actT, actTb = actTs[st % 2]
                nxt = st + 1 < NST
                for jc in range(NFC):
                    step = st * NFC + jc
                    if dn is not None:
                        dn.emit(step, 4)
                    run_events(step)
                    if nxt:
                        if jc == 2:
                            fr.load(st + 1)
                        if jc == 5:
                            fr.tr_begin(st + 1)
                        if jc == 6:
                            fr.norm_a(st + 1, 0)
                        if jc == 8:
                            fr.norm_a(st + 1, 1)
                            fr.norm_b(st + 1, 0)
                        if jc == 9:
                            fr.norm_c(st + 1, 0)
                        if jc == 10:
                            fr.norm_b(st + 1, 1)
                        if jc == 12:
                            fr.norm_c(st + 1, 1)
                        if jc == 15:
                            fr.tr_tile(st + 1, 0)
                        if jc == 18:
                            cur = fr.tr_tile(st + 1, 1)
                    pair = []
                    for gv in range(2):
                        bk = (uc % 4)
                        uc += 1
                        ch = gv * NFC + jc
                        wu, wub = wub_l[(jc // 4) * 2 + gv]
                        col = (jc % 4) * 128
                        for c in range(8):
                            k.op("pe", lambda e: e.matmul(ps[:, bk, 0:TS + 2], lhsT=wu[:, c, col:col + 128],
                                                          rhs=hnT[:, c, :], start=(c == 0), stop=(c == 7)),
                                 [wub, hnTb], [PB[bk]])
                        c_t, c_b = cv[uc % len(cv)]
                        k.op("act", lambda e: e.activation(out=c_t[:], in_=ps[:, bk, 2:TS + 2], func=AF.Identity,
                                                           scale=cw[:, 2, ch:ch + 1], bias=cb[:, ch:ch + 1]),
                             [PB[bk], cwb, cbb], [c_b])
                        for kk in (1, 0):
                            k.op("dve", lambda e: e.scalar_tensor_tensor(out=c_t[:], in0=ps[:, bk, kk:kk + TS],
                                                                         scalar=cw[:, kk, ch:ch + 1], in1=c_t[:],
                                                                         op0=ALU.mult, op1=ALU.add),
                                 [PB[bk], cwb, c_b], [c_b])
                        pair.append((c_t, c_b))
                    g_t, g_b = gl[jc % len(gl)]
                    k.op("act", lambda e: e.activation(out=g_t[:], in_=pair[0][0][:], func=AF.Gelu_apprx_tanh),
                         [pair[0][1]], [g_b])
                    k.op("pool", lambda e: e.tensor_tensor(out=actT[:, jc, :], in0=g_t[:], in1=pair[1][0][:],
                                                           op=ALU.mult), [g_b, pair[1][1]], [actTb])
                dn = Down(st, actT, actTb)
            step = NST * NFC
            while dn.m < 4 * NFC or events:
                dn.emit(step, 4)
                run_events(step)
                step += 1

    def phase_ret_proj(layer, hsrc):
        j = layer // 2
        with Phase(k) as P:
            wblk = load_w_blocks(P, "win", ret_w_in[j], 8, [(i_ * D, D) for i_ in range(6)])
            fr = Front(P, hsrc, g_mix_pre[layer], 7)
            qst = P.T("rqst", [128, 8, 512], BF16, dma=True)
            kst = P.T("rkst", [128, 8, 512], BF16, dma=True)
            vst = [P.T("rvst", [128, 2048], BF16, dma=True) for _ in range(2)]
            gst = [P.T("rgst", [128, 2048], BF16, dma=True) for _ in range(2)]
            cs_t, cs_b = P.T("cos", [128, 512], F32, dma=True)
            sn_t, sn_b = P.T("sin", [128, 512], F32, dma=True)
            tm = [P.T("rtm", [128, 512], F32) for _ in range(4)]
            fr.load(0)
            bc = 0
            ev = 0
            for st in range(8):
                cs = slice(st * 512, (st + 1) * 512)
                k.dma("sp", cs_t[:], c_cosT[:, cs], cs_b, writes=[cs_b])
                k.dma("sp", sn_t[:], c_sinT[:, cs], sn_b, writes=[sn_b])
                if st == 0:
                    cur = fr.emit(0)
                    fr.load(1)
                hnT, hnTb, _, _ = cur
                for which, ((s_t, s_b), dst) in enumerate(((qst, qT_d), (kst, kT_d))):
                    for h in range(4):
                        bks = []
                        for dc in range(2):
                            bk = bc % 6
                            bc += 1
                            bks.append(bk)
                            w, wb = wblk[which]
                            col = h * 256 + dc * 128
                            for c in range(8):
                                k.op("pe", lambda e: e.matmul(ps[:, bk, :], lhsT=w[:, c, col:col + 128],
                                                              rhs=hnT[:, c, :], start=(c == 0), stop=(c == 7)),
                                     [wb, hnTb], [PB[bk]])
                        x1, x2 = bks
                        for ti, (xb_, tab, tabb) in enumerate(((x1, cs_t, cs_b), (x2, sn_t, sn_b),
                                                               (x1, sn_t, sn_b), (x2, cs_t, cs_b))):
                            k.op("dve", lambda e: e.tensor_tensor(out=tm[ti][0][:], in0=ps[:, xb_, :], in1=tab[:],
                                                                  op=ALU.mult), [PB[xb_], tabb], [tm[ti][1]])
                        k.op("pool", lambda e: e.tensor_tensor(out=s_t[:, h * 2, :], in0=tm[0][0][:], in1=tm[1][0][:],
                                                               op=ALU.subtract), [tm[0][1], tm[1][1]], [s_b])
                        k.op("pool", lambda e: e.tensor_tensor(out=s_t[:, h * 2 + 1, :], in0=tm[2][0][:],
                                                               in1=tm[3][0][:], op=ALU.add),
                             [tm[2][1], tm[3][1]], [s_b])
                    k.dma("sp", dst.rearrange("h p t -> p h t")[:, :, cs], s_t[:], s_b, reads=[s_b])
                    if st + 1 < 8 and which == 0:
                        fr.emit_norm(st + 1)
                if st + 1 < 8:
                    cur = fr.emit_tr(st + 1)
                    if st + 2 < 8:
                        fr.load(st + 2)
                for t in range(4):
                    row = st * 4 + t
                    for which, (stg, dst) in enumerate(((vst, v_d), (gst, g_d))):
                        s_t, s_b = stg[row % 2]
                        for h in range(4):
                            bk = bc % 6
                            bc += 1
                            w, wb = wblk[2 + which * 2 + h // 2]
                            col = (h % 2) * 512
                            for c in range(8):
                                k.op("pe", lambda e: e.matmul(ps[:, bk, :], lhsT=hnT[:, c, t * 128:(t + 1) * 128],
                                                              rhs=w[:, c, col:col + 512],
                                                              start=(c == 0), stop=(c == 7)), [wb, hnTb], [PB[bk]])
                            if which == 1:
                                k.op("act", lambda e: e.activation(out=s_t[:, h * 512:(h + 1) * 512],
                                                                   in_=ps[:, bk, :], func=AF.Silu), [PB[bk]], [s_b])
                            else:
                                k.op("dve", lambda e: e.tensor_copy(out=s_t[:, h * 512:(h + 1) * 512],
                                                                    in_=ps[:, bk, :]), [PB[bk]], [s_b])
                        k.dma("sp", dst[row * 128:(row + 1) * 128, :], s_t[:], s_b, reads=[s_b])

    def phase_ret_rec(layer):
        j = layer // 2
        with Phase(k) as P:
            dpt, dptb = P.T("dpt", [128, 512], F32, dma=True)
            rc, rcb = P.T("rcol", [128, 16], F32, dma=True)
            gn, gnb = P.T("gncol", [128, 16], F32, dma=True)
            k.dma("sp", dpt[:], c_dpt[:, :], dptb, writes=[dptb])
            k.dma("sp", rc[:], c_rcol[:, :], rcb, writes=[rcb])
            k.dma("sp", gn[:], ret_gn[j], gnb, writes=[gnb])
            gnx, gnxb = P.T("gnx", [128, 16, 128], F32)
            k.op("pool", lambda e: e.memset(gnx[:], 1.0), [], [gnxb])
            for q_ in range(16):
                k.op("pool", lambda e: e.tensor_scalar_mul(out=gnx[:, q_, :], in0=gnx[:, q_, :],
                                                           scalar1=gn[:, q_:q_ + 1]), [gnb, gnxb], [gnxb])
            qg = [P.T("qg", [128, 8, 512], BF16, dma=True) for _ in range(2)]
            kg = [P.T("kg", [128, 8, 512], BF16, dma=True) for _ in range(2)]
            vg = [P.T("vg", [128, 4, 2048], BF16, dma=True) for _ in range(2)]
            gg = [P.T("gg", [128, 4, 2048], BF16, dma=True) for _ in range(2)]
            yT = [P.T("ryT", [128, 16, 512], BF16, dma=True) for _ in range(2)]
            St = [P.T("St", [128, 2, 512], F32) for _ in range(4)]
            Sb = [P.T("Sb", [128, 2, 512], BF16) for _ in range(4)]
            PT = [P.T("PT", [128, 128], BF16) for _ in range(4)]
            kt = [P.T("ktok", [128, 256], BF16) for _ in range(4)]
            NY = 4
            y1 = [P.T("y1", [128, 512], F32) for _ in range(NY)]
            yk = [P.T("ytok", [128, 512], BF16) for _ in range(NY)]
            st6 = [P.T("st6", [128, 1, 6], F32) for _ in range(NY)]
            mv = [P.T("mv", [128, 8], F32) for _ in range(NY)]

            def load(g):
                cs = slice(g * 512, (g + 1) * 512)
                for (t_, b_), src in ((qg[g % 2], qT_d), (kg[g % 2], kT_d)):
                    k.dma("sp", t_[:], src.rearrange("h p t -> p h t")[:, :, cs], b_, writes=[b_])
                for (t_, b_), src in ((vg[g % 2], v_d), (gg[g % 2], g_d)):
                    k.dma("sp", t_[:], src.rearrange("(t p) n -> p t n", p=128)[:, g * 4:(g + 1) * 4, :], b_,
                          writes=[b_])

            load(0)
            oc = 0
            sc = 0
            pend = []
            kt7 = psbf(7)
            kt1 = psbf(1)

            def flush_one():
                if pend:
                    fn, after = pend.pop(0)
                    fn()
                    if after is not None:
                        after()

            for g in range(8):
                if g + 1 < 8:
                    load(g + 1)
                q_t, q_b = qg[g % 2]
                k_t, k_b = kg[g % 2]
                v_t, v_b = vg[g % 2]
                g_t, g_b = gg[g % 2]
                y_t, y_b = yT[g % 2]
                for t in range(4):
                    T_ = g * 4 + t
                    tc = slice(t * 128, (t + 1) * 128)
                    last = (T_ == NT - 1)
                    for h in range(4):
                        for c in range(2):
                            k.op("pe", lambda e: e.matmul(ps[:, 0, h * 128:(h + 1) * 128], lhsT=k_t[:, h * 2 + c, tc],
                                                          rhs=q_t[:, h * 2 + c, tc], start=(c == 0), stop=(c == 1)),
                                 [k_b, q_b], [PB[0]])
                    if not last:
                        for h in range(4):
                            for c in range(2):
                                k.op("pe", lambda e: e.transpose(out=kt1[:, (h * 2 + c) * 128:(h * 2 + c + 1) * 128],
                                                                 in_=k_t[:, h * 2 + c, tc], identity=ident[:]),
                                     [k_b, identB], [PB[1]])
                    for h in range(4):
                        k.op("dve", lambda e: e.tensor_tensor(out=PT[h][0][:], in0=ps[:, 0, h * 128:(h + 1) * 128],
                                                              in1=dpt[:, h * 128:(h + 1) * 128], op=ALU.mult),
                             [PB[0], dptb], [PT[h][1]])
                        if not last:
                            k.op("act", lambda e: e.activation(out=kt[h][0][:], in_=kt1[:, h * 256:(h + 1) * 256],
                                                               func=AF.Identity, scale=rc[:, 8 + h:9 + h]),
                                 [PB[1], rcb], [kt[h][1]])
                    for h in range(4):
                        ob = 2 + oc % 3
                        oc += 1
                        vs = v_t[:, t, h * 512:(h + 1) * 512]
                        k.op("pe", lambda e: e.matmul(ps[:, ob, :], lhsT=PT[h][0][:], rhs=vs, start=True,
                                                      stop=(T_ == 0)), [PT[h][1], v_b], [PB[ob]])
                        if T_ > 0:
                            for c in range(2):
                                k.op("pe", lambda e: e.matmul(ps[:, ob, :], lhsT=q_t[:, h * 2 + c, tc],
                                                              rhs=Sb[h][0][:, c, :], start=False, stop=(c == 1)),
                                     [q_b, Sb[h][1]], [PB[ob]])
                        if not last:
                            for c in range(2):
                                k.op("pe", lambda e: e.matmul(ps[:, 5 + c, :], lhsT=kt[h][0][:, c * 128:(c + 1) * 128],
                                                              rhs=vs, start=True, stop=True),
                                     [kt[h][1], v_b], [PB[5 + c]])
                        s6, s6b = st6[sc % NY]
                        m, mb = mv[sc % NY]
                        y1t, y1b = y1[sc % NY]
                        ykt, ykb = yk[sc % NY]
                        sc += 1
                        k.op("dve", lambda e: e.bn_stats(out=s6[:, 0, :], in_=ps[:, ob, :]), [PB[ob]], [s6b])
                        k.op("dve", lambda e: e.bn_aggr(out=m[:, 0:2], in_=s6[:]), [s6b], [mb])
                        k.op("pool", lambda e: e.tensor_scalar(out=m[:, 2:3], in0=m[:, 1:2], scalar1=rc[:, 4 + h:5 + h],
                                                               scalar2=GN_EPS, op0=ALU.mult, op1=ALU.add),
                             [mb, rcb], [mb])
                        k.op("pool", lambda e: e.tensor_tensor(out=m[:, 4:5], in0=m[:, 2:3], in1=nhalf[:],
                                                               op=ALU.pow), [mb, nhalfB], [mb])
                        k.op("pool", lambda e: e.tensor_tensor(out=m[:, 5:6], in0=m[:, 4:5], in1=rc[:, h:h + 1],
                                                               op=ALU.mult), [mb, rcb], [mb])
                        k.op("pool", lambda e: e.tensor_tensor(out=m[:, 6:7], in0=m[:, 0:1], in1=m[:, 5:6],
                                                               op=ALU.mult), [mb], [mb])
                        k.op("pool", lambda e: e.tensor_scalar_mul(out=m[:, 6:7], in0=m[:, 6:7], scalar1=-1.0),
                             [mb], [mb])
                        k.op("act", lambda e: e.activation(out=y1t[:], in_=ps[:, ob, :], func=AF.Identity,
                                                           scale=m[:, 5:6], bias=m[:, 6:7]), [PB[ob], mb], [y1b])
                        k.op("pool", lambda e: e.tensor_tensor(out=ykt[:], in0=y1t[:],
                                                               in1=g_t[:, t, h * 512:(h + 1) * 512], op=ALU.mult),
                             [y1b, g_b], [ykb])
                        if not last:
                            if T_ == 0:
                                k.op("dve", lambda e: e.tensor_copy(out=St[h][0][:], in_=ps[:, 5:7, :]),
                                     [PB[5], PB[6]], [St[h][1]])
                            else:
                                k.op("dve", lambda e: e.scalar_tensor_tensor(out=St[h][0][:], in0=St[h][0][:],
                                                                             scalar=_SDEC[h], in1=ps[:, 5:7, :],
                                                                             op0=ALU.mult, op1=ALU.add),
                                     [PB[5], PB[6], St[h][1]], [St[h][1]])
                            k.op("act", lambda e: e.copy(out=Sb[h][0][:], in_=St[h][0][:]),
                                 [St[h][1]], [Sb[h][1]])

                        def later(h=h, ykt=ykt, ykb=ykb, y_t=y_t, y_b=y_b, tc=tc):
                            for q in range(4):
                                k.op("pe", lambda e: e.transpose(out=kt7[:, q * 128:(q + 1) * 128],
                                                                 in_=ykt[:, q * 128:(q + 1) * 128], identity=ident[:]),
                                     [ykb, identB], [PB[7]])
                            k.op("dve", lambda e: e.tensor_tensor(
                                out=y_t[:, h * 4:(h + 1) * 4, tc],
                                in0=kt7[:, 0:512].rearrange("p (q n) -> p q n", q=4),
                                in1=gnx[:, h * 4:(h + 1) * 4, :], op=ALU.mult), [PB[7], gnxb], [y_b])

                        after = None
                        if t == 3 and h == 3:
                            after = (lambda g=g, y_t=y_t, y_b=y_b:
                                     k.dma("sp", oT_d[g], y_t[:], y_b, reads=[y_b]))
                        pend.append((later, after))
                        if len(pend) > 2:
                            flush_one()
                if g == 7:
                    while pend:
                        flush_one()

    hcur = x
    for layer in layers:
        last = layer == layers[-1]
        if layer % 2 == 0:
            if want():
                phase_sb_proj(layer, hcur)
            if want():
                phase_sb_attn()
            w3, kc3, wdn_pre = sb_w_o[layer // 2], 8, True
        else:
            if want():
                phase_ret_proj(layer, hcur)
            if want():
                phase_ret_rec(layer)
            w3, kc3, wdn_pre = ret_w_o[layer // 2], 16, False
        do3, do4 = want(), want()
        with Phase(k) as PW:
            pre, loaders = ffn_weight_loaders(PW, layer, wdn_pre) if (do4 and PREFETCH_FFN) else ({}, [])
            if do3:
                phase_outproj(layer, w3, kc3, hcur, hs, background=loaders)
            else:
                for f_ in loaders:
                    f_()
            hcur = hs
            if do4:
                phase_ffn(layer, hs, y if last else hs, pre=pre)
    k.barrier([identB])
    k.nc_ref = nc
    return nc, k


def _host_inputs(inputs):
    c, _ = _consts()
    f = lambda a: np.ascontiguousarray(np.asarray(a, dtype=np.float32))
    shared = {
        "norm_mix_pre": f(inputs["norm_mix_pre"]), "norm_mix_post": f(inputs["norm_mix_post"]),
        "norm_ffn_pre": f(inputs["norm_ffn_pre"]), "norm_ffn_post": f(inputs["norm_ffn_post"]),
        "sb_w_qkv": f(inputs["sb_w_qkv"]), "sb_w_o": f(inputs["sb_w_o"]),
        "ret_w_in": f(inputs["ret_w_in"]), "ret_w_o": f(inputs["ret_w_o"]),
        "ret_gn": np.ascontiguousarray(f(inputs["ret_gn"]).reshape(2, 16, 128).transpose(0, 2, 1)),
        "ffn_w_up": f(inputs["ffn_w_up"]), "ffn_w_down": f(inputs["ffn_w_down"]),
        "ffn_cw": np.ascontiguousarray(f(inputs["ffn_conv_w"]).reshape(DEPTH, 3, 44, 128).transpose(0, 3, 1, 2)),
        "ffn_cb": np.ascontiguousarray(f(inputs["ffn_conv_b"]).reshape(DEPTH, 44, 128).transpose(0, 2, 1)),
        "c_ident": c["ident"], "c_negtri": c["negtri"], "c_selk": c["selk"], "c_zero": c["zero"], "c_esel": c["esel"],
        "c_masks": c["masks"], "c_cosT": c["cosT"], "c_sinT": c["sinT"], "c_dpt": c["dpt"], "c_rcol": c["rcol"],
    }
    return shared


def kernel(**inputs):
    xs = np.asarray(inputs["x"], dtype=np.float32)
    shared = _host_inputs(inputs)
    nc, _ = build()
    in_maps = [dict(shared, x=np.ascontiguousarray(xs[b])) for b in range(8)]
    res = run_bass_kernel_spmd(nc, in_maps, core_ids=list(range(8)))
    return np.stack([np.asarray(r["y"], dtype=np.float32) for r in res.results], axis=0)
```

```python
import math
from contextlib import ExitStack

import numpy as np
import ml_dtypes

import concourse.bass as bass
import concourse.mybir as mybir
from concourse.bass_utils import run_bass_kernel_spmd

F32 = mybir.dt.float32
BF16 = mybir.dt.bfloat16
AF = mybir.ActivationFunctionType
ALU = mybir.AluOpType

D = 1024
S = 4096
NT = S // 128
DEPTH = 4
DFF = 2816
NFC = DFF // 128
EPS = 1e-6
GN_EPS = 1e-5
LAG = 2
S2_LAG = 5
PREFETCH_FFN = False


class Buf:
    __slots__ = ("name", "w", "r", "sem", "semval", "slot")

    def __init__(self, name):
        self.name = name
        self.w = None
        self.r = {}
        self.sem = None
        self.semval = 0
        self.slot = None


class K:
    def __init__(self, nc, n_dma_sems=64):
        self.nc = nc
        self.stack = ExitStack()
        self.engs = {"pe": nc.tensor, "act": nc.scalar, "dve": nc.vector, "pool": nc.gpsimd, "sp": nc.sync}
        self.esem, self.ecnt = {}, {}
        self.waited = {e: {} for e in self.engs}
        for e in self.engs:
            self.esem[e] = self.stack.enter_context(nc.semaphore("es_" + e))
            self.ecnt[e] = 0
        self.sempool = [[self.stack.enter_context(nc.semaphore("ds%d" % i)), 0] for i in range(n_dma_sems)]
        self.free_slots = list(range(n_dma_sems))
        self.n_inst = 0
        self.uid = 0

    def buf(self, name, dma=False):
        b = Buf(name)
        if dma:
            b.slot = self.free_slots.pop()
            b.sem, b.semval = self.sempool[b.slot]
        return b

    def release(self, b):
        if b.slot is not None:
            self.sempool[b.slot][1] = b.semval
            self.free_slots.append(b.slot)
            b.slot = None

    def _wait(self, eng, ticket):
        if ticket is None:
            return
        sem, val = ticket
        if eng == "pe" and sem is self.esem["pe"]:
            return
        key = id(sem)
        if self.waited[eng].get(key, 0) >= val:
            return
        self.waited[eng][key] = val
        self.engs[eng].wait_ge(sem, val)
        self.n_inst += 1

    def op(self, eng, fn, reads=(), writes=()):
        for b in reads:
            self._wait(eng, b.w)
        for b in writes:
            self._wait(eng, b.w)
            for t in list(b.r.values()):
                self._wait(eng, t)
        inst = fn(self.engs[eng])
        self.ecnt[eng] += 1
        sem = self.esem[eng]
        inst.then_inc(sem, 1)
        self.n_inst += 1
        t = (sem, self.ecnt[eng])
        for b in reads:
            b.r[id(sem)] = t
        for b in writes:
            b.w = t
            b.r = {}
        return t

    def dma(self, q, out, in_, sb, reads=(), writes=(), **kw):
        for b in reads:
            self._wait(q, b.w)
        for b in writes:
            if not (b.w is not None and b is sb and b.w[0] is sb.sem):
                self._wait(q, b.w)
            for t in list(b.r.values()):
                self._wait(q, t)
        inst = self.engs[q].dma_start(out=out, in_=in_, **kw)
        sb.semval += 16
        inst.then_inc(sb.sem, 16)
        self.n_inst += 1
        t = (sb.sem, sb.semval)
        for b in reads:
            b.r[id(sb.sem)] = t
        for b in writes:
            b.w = t
            b.r = {}
        return t

    def barrier(self, bufs=()):
        tickets = [(self.esem[e], self.ecnt[e]) for e in self.engs if self.ecnt[e] > 0]
        tickets += [(b.sem, b.semval) for b in bufs if b.sem is not None and b.semval > 0]
        for e in self.engs:
            for t in tickets:
                if t[0] is self.esem[e]:
                    continue
                self._wait(e, t)


class Phase:
    def __init__(self, k):
        self.k = k
        self.nc = k.nc

    def __enter__(self):
        self.st = ExitStack()
        self.bufs = []
        return self

    def T(self, name, shape, dt, dma=False):
        self.k.uid += 1
        t = self.st.enter_context(self.nc.sbuf_tensor("%s_%d" % (name, self.k.uid), list(shape), dt))
        b = self.k.buf(name, dma)
        self.bufs.append(b)
        return t, b

    def __exit__(self, *a):
        self.k.barrier(self.bufs)
        for b in self.bufs:
            self.k.release(b)
        self.st.close()
        return False


def _consts():
    bf = ml_dtypes.bfloat16
    c = {}
    c["ident"] = np.eye(128, dtype=np.float32).astype(bf)
    j = np.arange(128)[:, None]
    s = np.arange(128)[None, :]
    c["negtri"] = (-(j >= s).astype(np.float32)).astype(bf)
    bp = np.arange(32)[:, None, None]
    b = np.arange(32)[None, :, None]
    selk = np.zeros((64, S), np.float32)
    selk[0:32] = -(np.arange(32)[:, None] > (np.arange(S)[None, :] // 128)).astype(np.float32)
    c["selk"] = selk.astype(bf)
    c["zero"] = np.zeros((64, S), np.float32).astype(bf)
    es = np.zeros((128, 32, 128), np.float32)
    for i in range(32):
        es[:, i, i] = 1.0
        es[:, i, 64 + i] = 1.0
    c["esel"] = es.reshape(128, 32 * 128).astype(bf)
    p = np.arange(128)[:, None]
    xx = np.arange(512)[None, :]
    c["masks"] = np.concatenate([((xx - 128 * kk - p) > 0).astype(np.float32) for kk in range(4)], 1).astype(bf)
    half = 128
    inv_freq = (np.float32(10000.0) ** (-np.arange(half, dtype=np.float32) / np.float32(half))).astype(np.float32)
    pos = np.arange(S, dtype=np.float32)
    ang = (pos[None, :] * inv_freq[:, None]).astype(np.float32)
    c["cosT"] = np.cos(ang.astype(np.float64)).astype(np.float32)
    c["sinT"] = np.sin(ang.astype(np.float64)).astype(np.float32)
    i = np.arange(128)
    dpt = np.zeros((128, 4, 128), np.float64)
    rc = np.zeros((128, 16), np.float64)
    sdec = []
    for h in range(4):
        lg = np.log1p(-np.exp2(-5.0 - h))
        a = np.exp(lg * (i + 1.0))
        ci, cj = i[:, None] // 64, i[None, :] // 64
        li, lj = i[:, None] % 64, i[None, :] % 64
        Dm = np.where(ci == cj, np.exp(lg * np.abs(li - lj)),
                      np.where(ci > cj, np.exp(lg * (i[:, None] - i[None, :]).clip(0)), 0.0))
        dpt[:, h, :] = (Dm / a[:, None]).T * (256.0 ** -0.5)
        rc[:, h] = a
        rc[:, 4 + h] = a * a
        rc[:, 8 + h] = np.exp(lg * (127.0 - i)) * (256.0 ** -0.5)
        sdec.append(float(np.exp(lg * 128.0)))
    c["dpt"] = dpt.reshape(128, 512).astype(np.float32)
    c["rcol"] = rc.astype(np.float32)
    return c, sdec


_SDEC = [float(np.exp(np.log1p(-np.exp2(-5.0 - h)) * 128.0)) for h in range(4)]


def build(n_sub=None, dbg=False, layers=(0, 1, 2, 3)):
    nc = bass.Bass("TRN2", target_bir_lowering=False)

    def din(name, shape, dt=F32):
        return nc.dram_tensor(name, list(shape), dt, kind="ExternalInput").ap()

    def dscr(name, shape, dt):
        if dbg:
            return nc.dram_tensor(name, list(shape), dt, kind="ExternalOutput").ap()
        return nc.dram_tensor(name, list(shape), dt).ap()

    x = din("x", [S, D])
    g_mix_pre = din("norm_mix_pre", [DEPTH, D])
    g_mix_post = din("norm_mix_post", [DEPTH, D])
    g_ffn_pre = din("norm_ffn_pre", [DEPTH, D])
    g_ffn_post = din("norm_ffn_post", [DEPTH, D])
    sb_w_qkv = din("sb_w_qkv", [2, D, 3072])
    sb_w_o = din("sb_w_o", [2, D, D])
    ret_w_in = din("ret_w_in", [2, D, 6144])
    ret_gn = din("ret_gn", [2, 128, 16])
    ret_w_o = din("ret_w_o", [2, 2048, D])
    ffn_w_up = din("ffn_w_up", [DEPTH, D, 2 * DFF])
    ffn_cw = din("ffn_cw", [DEPTH, 128, 3, 44])
    ffn_cb = din("ffn_cb", [DEPTH, 128, 44])
    ffn_w_down = din("ffn_w_down", [DEPTH, DFF, D])
    c_ident = din("c_ident", [128, 128], BF16)
    c_negtri = din("c_negtri", [128, 128], BF16)
    c_selk = din("c_selk", [64, S], BF16)
    c_zero = din("c_zero", [64, S], BF16)
    c_esel = din("c_esel", [128, 32 * 128], BF16)
    c_masks = din("c_masks", [128, 4 * 512], BF16)
    c_cosT = din("c_cosT", [128, S])
    c_sinT = din("c_sinT", [128, S])
    c_dpt = din("c_dpt", [128, 512])
    c_rcol = din("c_rcol", [128, 16])
    y = nc.dram_tensor("y", [S, D], F32, kind="ExternalOutput").ap()

    hs = dscr("hs", [S, D], F32)
    qT_d = dscr("qT_d", [8, 128, S], BF16)
    kT_d = dscr("kT_d", [8, 128, S], BF16)
    v_d = dscr("v_d", [S, 2048], BF16)
    g_d = dscr("g_d", [S, 2048], BF16)
    oT_d = dscr("oT_d", [8, 128, 16, 512], BF16)

    k = K(nc)
    gs = k.stack
    ps = gs.enter_context(nc.psum_tensor("ps", [128, 8, 512], F32))
    PB = [k.buf("psb%d" % i) for i in range(8)]
    ident = gs.enter_context(nc.sbuf_tensor("ident", [128, 128], BF16))
    identB = k.buf("ident", dma=True)
    k.dma("sp", ident[:], c_ident[:, :], identB, writes=[identB])

    nhalf = gs.enter_context(nc.sbuf_tensor("nhalf", [128, 1], F32))
    nhalfB = k.buf("nhalf")
    k.op("pool", lambda e: e.memset(nhalf[:], -0.5), [], [nhalfB])

    def rstd_pool(out_ap, in_ap, buf, scale):
        k.op("pool", lambda e: e.tensor_scalar(out=out_ap, in0=in_ap, scalar1=scale, scalar2=EPS,
                                               op0=ALU.mult, op1=ALU.add), [buf], [buf])
        k.op("pool", lambda e: e.tensor_tensor(out=out_ap, in0=out_ap, in1=nhalf[:], op=ALU.pow),
             [buf, nhalfB], [buf])

    def psbf(bank):
        return ps[:, bank, :].bitcast(BF16)

    sub = [0]

    def want():
        sub[0] += 1
        return n_sub is None or sub[0] <= n_sub

    def load_w(P, name, src, nchunk, ncols, col0=0):
        t, b = P.T(name, [128, nchunk, ncols], BF16, dma=True)
        v = src.rearrange("(c p) n -> p c n", p=128)
        for c in range(nchunk):
            k.dma("pool", t[:, c, :], v[:, c, col0:col0 + ncols], b, writes=[b], max_dma_last_dim=4096)
        return t, b

    def load_w_blocks(P, name, src, nchunk, blocks):
        out = []
        v = src.rearrange("(c p) n -> p c n", p=128)
        for (col0, ncols) in blocks:
            t, b = P.T(name, [128, nchunk, ncols], BF16, dma=True)
            k.dma("pool", t[:], v[:, :, col0:col0 + ncols], b, writes=[b], max_dma_last_dim=4096)
            out.append((t, b))
        return out

    def load_bcast(P, name, vec):
        t, b = P.T(name, [128, D], F32, dma=True)
        k.dma("sp", t[:], vec.partition_broadcast(128), b, writes=[b])
        return t, b

    class Front:
        def __init__(self, P, hsrc, gvec, trbank, hb_n=1, nt=4, halo=0, use_pool=False):
            self.P, self.hsrc, self.trbank, self.nt, self.halo = P, hsrc, trbank, nt, halo
            self.use_pool = use_pool
            self.hb = [P.T("hb", [128, nt, D], F32, dma=True) for _ in range(hb_n)]
            self.hnT = [P.T("hnT", [128, 8, halo + nt * 128], BF16) for _ in range(2)]
            self.hnb = [P.T("hnb", [128, D], BF16) for _ in range(nt)]
            self.junk = P.T("junk", [128, D], BF16)
            self.stat = [P.T("stat", [128, 4], F32) for _ in range(4)]
            self.g = load_bcast(P, "gpre", gvec)

        def load(self, st):
            t, b = self.hb[st % len(self.hb)]
            src = self.hsrc.rearrange("(t p) n -> p t n", p=128)[:, st * self.nt:(st + 1) * self.nt, :]
            k.dma("sp", t[:], src, b, writes=[b])

        def norm_a(self, st, t):
            hb_t, hb_b = self.hb[st % len(self.hb)]
            jk_t, jk_b = self.junk
            sm, smb = self.stat[t]
            k.op("act", lambda e: e.activation(out=jk_t[:], in_=hb_t[:, t, :], func=AF.Square,
                                               accum_out=sm[:, 0:1]), [hb_b], [jk_b, smb])

        def norm_b(self, st, t):
            sm, smb = self.stat[t]
            if self.use_pool:
                rstd_pool(sm[:, 2:3], sm[:, 0:1], smb, 1.0 / D)
            else:
                k.op("act", lambda e: e.activation(out=sm[:, 1:2], in_=sm[:, 0:1], func=AF.Sqrt,
                                                   scale=1.0 / D, bias=EPS), [smb], [smb])
                k.op("dve", lambda e: e.reciprocal(out=sm[:, 2:3], in_=sm[:, 1:2]), [smb], [smb])

        def norm_c(self, st, t):
            hb_t, hb_b = self.hb[st % len(self.hb)]
            g_t, g_b = self.g
            sm, smb = self.stat[t]
            hn, hnb_ = self.hnb[t]
            k.op("dve", lambda e: e.scalar_tensor_tensor(out=hn[:], in0=hb_t[:, t, :], scalar=sm[:, 2:3],
                                                         in1=g_t[:], op0=ALU.mult, op1=ALU.mult),
                 [hb_b, smb, g_b], [hnb_])

        def emit_norm(self, st, tiles=None):
            for t in (range(self.nt) if tiles is None else tiles):
                self.norm_a(st, t)
                self.norm_b(st, t)
                self.norm_c(st, t)

        def tr_begin(self, st):
            hnT_t, hnT_b = self.hnT[st % 2]
            H = self.halo
            if H:
                if st == 0:
                    k.op("dve", lambda e: e.memset(hnT_t[:, :, 0:H], 0.0), [], [hnT_b])
                else:
                    p_t, p_b = self.hnT[(st - 1) % 2]
                    k.op("dve", lambda e: e.tensor_copy(out=hnT_t[:, :, 0:H],
                                                         in_=p_t[:, :, self.nt * 128:self.nt * 128 + H]),
                         [p_b], [hnT_b])

        def tr_tile(self, st, t):
            hb_t, hb_b = self.hb[st % len(self.hb)]
            hnT_t, hnT_b = self.hnT[st % 2]
            trp = psbf(self.trbank)
            H = self.halo
            hn, hnb_ = self.hnb[t]
            for c in range(8):
                k.op("pe", lambda e: e.transpose(out=trp[:, c * 128:(c + 1) * 128],
                                                 in_=hn[:, c * 128:(c + 1) * 128], identity=ident[:]),
                     [hnb_, identB], [PB[self.trbank]])
            k.op("dve", lambda e: e.tensor_copy(out=hnT_t[:, :, H + t * 128:H + (t + 1) * 128],
                                                in_=trp.rearrange("p (c n) -> p c n", c=8)),
                 [PB[self.trbank]], [hnT_b])
            return hnT_t, hnT_b, hb_t, hb_b

        def emit_tr(self, st):
            self.tr_begin(st)
            for t in range(self.nt):
                r = self.tr_tile(st, t)
            return r

        def emit(self, st):
            self.emit_norm(st)
            return self.emit_tr(st)

    class Post:
        def __init__(self, P, gvec, ntmp=2, split_add=False, use_pool=False):
            self.split_add = split_add
            self.use_pool = use_pool
            self.g = load_bcast(P, "gpost", gvec)
            self.junk = P.T("pjunk", [128, 512], BF16)
            self.tmp = [P.T("ptmp", [128, D], F32) for _ in range(ntmp)]
            self.stat = [P.T("pstat", [128, 8], F32) for _ in range(2)]
            self.n = 0

        def stage_a(self, banks):
            jk, jkb = self.junk
            tmp, tmpb = self.tmp[self.n % len(self.tmp)]
            sm, smb = self.stat[self.n % 2]
            self.n += 1
            for hf in range(2):
                k.op("act", lambda e: e.activation(out=jk[:], in_=ps[:, banks[hf], :], func=AF.Square,
                                                   accum_out=sm[:, hf:hf + 1]), [PB[banks[hf]]], [jkb, smb])
            return {"banks": banks, "tmp": tmp, "tmpb": tmpb, "sm": sm, "smb": smb}

        def stage_b(self, c):
            sm, smb = c["sm"], c["smb"]
            if self.use_pool:
                k.op("pool", lambda e: e.tensor_tensor(out=sm[:, 2:3], in0=sm[:, 0:1], in1=sm[:, 1:2], op=ALU.add),
                     [smb], [smb])
                rstd_pool(sm[:, 4:5], sm[:, 2:3], smb, 1.0 / D)
            else:
                k.op("dve", lambda e: e.tensor_tensor(out=sm[:, 2:3], in0=sm[:, 0:1], in1=sm[:, 1:2], op=ALU.add),
                     [smb], [smb])
                k.op("act", lambda e: e.activation(out=sm[:, 3:4], in_=sm[:, 2:3], func=AF.Sqrt,
                                                   scale=1.0 / D, bias=EPS), [smb], [smb])
                k.op("dve", lambda e: e.reciprocal(out=sm[:, 4:5], in_=sm[:, 3:4]), [smb], [smb])

        def stage_c(self, c, h_ap, h_b):
            g_t, g_b = self.g
            banks, tmp, tmpb, sm, smb = c["banks"], c["tmp"], c["tmpb"], c["sm"], c["smb"]
            for hf in range(2):
                k.op("dve", lambda e: e.scalar_tensor_tensor(out=tmp[:, hf * 512:(hf + 1) * 512],
                                                             in0=ps[:, banks[hf], :], scalar=sm[:, 4:5],
                                                             in1=g_t[:, hf * 512:(hf + 1) * 512],
                                                             op0=ALU.mult, op1=ALU.mult),
                     [PB[banks[hf]], smb, g_b], [tmpb])
            if self.split_add:
                k.op("pool", lambda e: e.tensor_tensor(out=h_ap[:, 0:512], in0=h_ap[:, 0:512], in1=tmp[:, 0:512],
                                                       op=ALU.add), [tmpb, h_b], [h_b])
                k.op("dve", lambda e: e.tensor_tensor(out=h_ap[:, 512:D], in0=h_ap[:, 512:D], in1=tmp[:, 512:D],
                                                      op=ALU.add), [tmpb, h_b], [h_b])
            else:
                k.op("pool", lambda e: e.tensor_tensor(out=h_ap, in0=h_ap, in1=tmp[:], op=ALU.add),
                     [tmpb, h_b], [h_b])

        def emit(self, banks, h_ap, h_b):
            c = self.stage_a(banks)
            self.stage_b(c)
            self.stage_c(c, h_ap, h_b)

    def phase_sb_proj(layer, hsrc):
        j = layer // 2
        with Phase(k) as P:
            wblk = load_w_blocks(P, "wqkv", sb_w_qkv[j], 8, [(0, D), (D, D), (2 * D, D)])
            fr = Front(P, hsrc, g_mix_pre[layer], 7)
            qst = [P.T("qst", [128, 8, 512], BF16, dma=True) for _ in range(2)]
            kst = [P.T("kst", [128, 8, 512], BF16, dma=True) for _ in range(2)]
            vst = [P.T("vst", [128, 4, D], BF16, dma=True) for _ in range(2)]
            fr.load(0)
            bank = [0]

            def nb():
                bank[0] = (bank[0] + 1) % 6
                return bank[0]

            ev = [0]
            cur = fr.emit(0)
            fr.load(1)
            for st in range(8):
                hnT, hnTb, _, _ = cur
                cs = slice(st * 512, (st + 1) * 512)
                for which, (stg, dst, scale) in enumerate(((qst, qT_d, 0.125), (kst, kT_d, 1.0))):
                    s_t, s_b = stg[st % 2]
                    w, wb = wblk[which]
                    for hp in range(8):
                        bk = nb()
                        for c in range(8):
                            k.op("pe", lambda e: e.matmul(ps[:, bk, :],
                                                          lhsT=w[:, c, hp * 128:(hp + 1) * 128],
                                                          rhs=hnT[:, c, :], start=(c == 0), stop=(c == 7)),
                                 [wb, hnTb], [PB[bk]])
                        ev[0] += 1
                        if ev[0] % 2:
                            k.op("act", lambda e: e.mul(out=s_t[:, hp, :], in_=ps[:, bk, :], mul=scale),
                                 [PB[bk]], [s_b])
                        else:
                            k.op("dve", lambda e: e.tensor_scalar_mul(out=s_t[:, hp, :], in0=ps[:, bk, :],
                                                                      scalar1=scale), [PB[bk]], [s_b])
                    k.dma("sp", dst.rearrange("h p t -> p h t")[:, :, cs], s_t[:], s_b, reads=[s_b])
                    if st + 1 < 8 and which == 0:
                        fr.emit_norm(st + 1)
                if st + 1 < 8:
                    cur = fr.emit_tr(st + 1)
                    if st + 2 < 8:
                        fr.load(st + 2)
                v_t, v_b = vst[st % 2]
                w, wb = wblk[2]
                for t in range(4):
                    for hf in range(2):
                        bk = nb()
                        for c in range(8):
                            k.op("pe", lambda e: e.matmul(ps[:, bk, :], lhsT=hnT[:, c, t * 128:(t + 1) * 128],
                                                          rhs=w[:, c, hf * 512:(hf + 1) * 512],
                                                          start=(c == 0), stop=(c == 7)), [wb, hnTb], [PB[bk]])
                        ev[0] += 1
                        eng = "act" if ev[0] % 2 else "dve"
                        if eng == "act":
                            k.op("act", lambda e: e.copy(out=v_t[:, t, hf * 512:(hf + 1) * 512], in_=ps[:, bk, :]),
                                 [PB[bk]], [v_b])
                        else:
                            k.op("dve", lambda e: e.tensor_copy(out=v_t[:, t, hf * 512:(hf + 1) * 512],
                                                                in_=ps[:, bk, :]), [PB[bk]], [v_b])
                k.dma("sp", v_d.rearrange("(t p) n -> p t n", p=128)[:, st * 4:(st + 1) * 4, 0:D], v_t[:], v_b,
                      reads=[v_b])

    def phase_sb_attn():
        with Phase(k) as P:
            tri, trib = P.T("tri", [128, 128], BF16, dma=True)
            esel, eselb = P.T("esel", [128, 32 * 128], BF16, dma=True)
            msk, mskb = P.T("msk", [128, 4 * 512], BF16, dma=True)
            k.dma("sp", tri[:], c_negtri[:, :], trib, writes=[trib])
            k.dma("sp", esel[:], c_esel[:, :], eselb, writes=[eselb])
            k.dma("sp", msk[:], c_masks[:, :], mskb, writes=[mskb])
            qp = [[P.T("qp", [128, S], BF16, dma=True) for _e in range(2)] for _ in range(2)]
            kp = [[P.T("kp", [128, S], BF16, dma=True) for _e in range(2)] for _ in range(2)]
            vp = [P.T("vp", [128, NT, 128], BF16, dma=True) for _ in range(2)]
            oT = [P.T("oT", [128, S], BF16, dma=True) for _ in range(2)]
            Ls = [P.T("Ls%d" % i, [128, 32, 512], BF16)[0] for i in range(2)]
            LB = [[k.buf("L%d_%d" % (i, b_)) for b_ in range(32)] for i in range(2)]
            Asb = [P.T("Asb", [128, 512], BF16) for _ in range(6)]
            oh = [slice(64, 128), slice(0, 64)]
            hh = [slice(0, 64), slice(64, 128)]
            for i_ in range(2):
                for e_ in range(2):
                    k.dma("sp", kp[i_][e_][0][oh[e_], :], c_selk[:, :], kp[i_][e_][1], writes=[kp[i_][e_][1]])

            def load(hp):
                for e_ in range(2):
                    q_t, q_b = qp[hp % 2][e_]
                    k_t, k_b = kp[hp % 2][e_]
                    k.dma("sp", q_t[hh[e_], :], qT_d[hp, hh[e_], :], q_b, writes=[q_b])
                    k.dma("sp", q_t[oh[e_], :], c_zero[:, :], q_b, writes=[q_b])
                    k.dma("sp", k_t[hh[e_], :], kT_d[hp, hh[e_], :], k_b, writes=[k_b])
                k.dma("sp", vp[hp % 2][0][:],
                      v_d.rearrange("(t p) n -> p t n", p=128)[:, :, hp * 128:(hp + 1) * 128],
                      vp[hp % 2][1], writes=[vp[hp % 2][1]])

            load(0)
            cnt = {"w": 0, "a": 0}
            items = []

            def pass1(hp, J, e_, off):
                nb = 4 * J + 4
                cs = slice(J * 512, (J + 1) * 512)
                q_t, q_b = qp[hp % 2][e_]
                k_t, k_b = kp[hp % 2][e_]
                L, Lb = Ls[e_], LB[e_]
                for i in range(nb):
                    st_ = {}

                    def produce(i=i, st_=st_):
                        zb = cnt["w"] % 6
                        cnt["w"] += 1
                        st_["b"] = zb
                        c0 = 128 * max(0, i - 4 * J) if hp > 0 else 0
                        k.op("pe", lambda e: e.matmul(ps[:, zb, c0:512], lhsT=k_t[:, i * 128:(i + 1) * 128],
                                                      rhs=q_t[:, J * 512 + c0:(J + 1) * 512], start=True, stop=True),
                             [k_b, q_b], [PB[zb]])

                    def consume(jb=i, st_=st_):
                        zb = st_["b"]
                        c0 = 128 * max(0, jb - 4 * J) if hp > 0 else 0
                        k.op("act", lambda e: e.activation(out=L[:, off + jb, c0:512], in_=ps[:, zb, c0:512],
                                                           func=AF.Softplus), [PB[zb]], [Lb[off + jb]])
                        kk = jb - 4 * J
                        if kk >= 0:
                            k.op("dve", lambda e: e.tensor_tensor(out=L[:, off + jb, :], in0=L[:, off + jb, :],
                                                                  in1=msk[:, kk * 512:(kk + 1) * 512], op=ALU.mult),
                                 [mskb, Lb[off + jb]], [Lb[off + jb]])
                        k.op("pe", lambda e: e.matmul(ps[:, 6, :], lhsT=esel[:, jb * 128:(jb + 1) * 128],
                                                      rhs=L[:, off + jb, :], start=(jb == 0), stop=(jb == nb - 1)),
                             [eselb, Lb[off + jb]], [PB[6]])
                        if jb == nb - 1:
                            so = 64 if e_ == 0 else 0
                            k.op("dve", lambda e: e.tensor_copy(out=q_t[so:so + 32, cs], in_=ps[so:so + 32, 6, :]),
                                 [PB[6]], [q_b])

                    items.append((produce, consume))

            def pass2(hp, J, e_, off):
                nb = 4 * J + 4
                cs = slice(J * 512, (J + 1) * 512)
                q_t, q_b = qp[hp % 2][e_]
                k_t, k_b = kp[hp % 2][e_]
                v_t, v_b = vp[hp % 2]
                o_t, o_b = oT[hp % 2]
                L, Lb = Ls[e_], LB[e_]
                for i in range(nb):
                    st_ = {}

                    def produce(i=i, st_=st_):
                        pb = cnt["w"] % 6
                        cnt["w"] += 1
                        st_["b"] = pb
                        c0 = 128 * max(0, i - 4 * J) if hp > 0 else 0
                        k.op("pe", lambda e: e.matmul(ps[:, pb, c0:512], lhsT=k_t[:, i * 128:(i + 1) * 128],
                                                      rhs=q_t[:, J * 512 + c0:(J + 1) * 512], start=True, stop=False),
                             [k_b, q_b], [PB[pb]])
                        k.op("pe", lambda e: e.matmul(ps[:, pb, c0:512], lhsT=tri[:], rhs=L[:, off + i, c0:512],
                                                      start=False, stop=True), [trib, Lb[off + i]], [PB[pb]])

                    def consume(jb=i, st_=st_):
                        pb = st_["b"]
                        a_t, a_b = Asb[cnt["a"] % 6]
                        cnt["a"] += 1
                        c0 = 128 * max(0, jb - 4 * J) if hp > 0 else 0
                        k.op("act", lambda e: e.activation(out=a_t[:, c0:512], in_=ps[:, pb, c0:512], func=AF.Exp),
                             [PB[pb]], [a_b])
                        kk = jb - 4 * J
                        if kk >= 0:
                            k.op("dve", lambda e: e.tensor_tensor(out=a_t[:], in0=a_t[:],
                                                                  in1=msk[:, kk * 512:(kk + 1) * 512], op=ALU.mult),
                                 [mskb, a_b], [a_b])
                        k.op("pe", lambda e: e.matmul(ps[:, 7, :], lhsT=v_t[:, jb, :], rhs=a_t[:],
                                                      start=(jb == 0), stop=(jb == nb - 1)), [v_b, a_b], [PB[7]])
                        if jb == nb - 1:
                            k.op("dve", lambda e: e.tensor_copy(out=o_t[hh[e_], cs], in_=ps[hh[e_], 7, :]),
                                 [PB[7]], [o_b])
                            if e_ == 1 and J == 7:
                                k.dma("sp", oT_d.rearrange("g p c t -> p g c t")[:, :, hp, :],
                                      o_t.rearrange("p (g t) -> p g t", g=8), o_b, reads=[o_b])
                                if hp + 2 < 8:
                                    load(hp + 2)

                    items.append((produce, consume))

            load(1)
            jsets = [((0, 0), (6, 4)), ((1, 0), (5, 8)), ((2, 0), (4, 12)), ((3, 0),), ((7, 0),)]
            for hp in range(8):
                for js in jsets:
                    for (J, off) in js:
                        pass1(hp, J, 0, off)
                        pass1(hp, J, 1, off)
                    for (J, off) in js:
                        pass2(hp, J, 0, off)
                        pass2(hp, J, 1, off)
            for n in range(len(items) + S2_LAG):
                if n < len(items):
                    items[n][0]()
                if n - S2_LAG >= 0:
                    items[n - S2_LAG][1]()

    def phase_outproj(layer, w_src, kc, hsrc, hdst, background=()):
        background = list(background)
        with Phase(k) as P:
            w, wb = load_w(P, "wo", w_src, kc, D)
            post = Post(P, g_mix_post[layer], split_add=True)
            yT = [P.T("yT", [128, kc, 512], BF16, dma=True) for _ in range(2)]
            hb = [P.T("hbo", [128, 4, D], F32, dma=True) for _ in range(2)]

            def load(g):
                cs = slice(g * 512, (g + 1) * 512)
                k.dma("sp", yT[g % 2][0][:], oT_d[g, :, 0:kc, :], yT[g % 2][1], writes=[yT[g % 2][1]])
                k.dma("sp", hb[g % 2][0][:], hsrc.rearrange("(t p) n -> p t n", p=128)[:, g * 4:(g + 1) * 4, :],
                      hb[g % 2][1], writes=[hb[g % 2][1]])

            load(0)
            bc = 0
            for g in range(8):
                if g + 1 < 8:
                    load(g + 1)
                y_t, y_b = yT[g % 2]
                h_t, h_b = hb[g % 2]
                for t in range(4):
                    banks = []
                    for hf in range(2):
                        bk = bc % 8
                        bc += 1
                        banks.append(bk)
                        for c in range(kc):
                            k.op("pe", lambda e: e.matmul(ps[:, bk, :], lhsT=y_t[:, c, t * 128:(t + 1) * 128],
                                                          rhs=w[:, c, hf * 512:(hf + 1) * 512],
                                                          start=(c == 0), stop=(c == kc - 1)), [wb, y_b], [PB[bk]])
                    post.emit(banks, h_t[:, t, :], h_b)
                    if background:
                        background.pop(0)()
                k.dma("sp", hdst.rearrange("(t p) n -> p t n", p=128)[:, g * 4:(g + 1) * 4, :], h_t[:], h_b,
                      reads=[h_b])
            while background:
                background.pop(0)()

    def ffn_weight_loaders(PW, layer, with_down):
        wu, wub = PW.T("wup", [128, 8, 2 * DFF], BF16, dma=True)
        vu = ffn_w_up[layer].rearrange("(c p) n -> p c n", p=128)
        loaders = [(lambda c=c: k.dma("pool", wu[:, c, :], vu[:, c, :], wub, writes=[wub], max_dma_last_dim=4096))
                   for c in range(8)]
        pre = {"wu": (wu, wub)}
        if with_down:
            wd, wdb = PW.T("wdn", [128, NFC, D], BF16, dma=True)
            vd = ffn_w_down[layer].rearrange("(c p) n -> p c n", p=128)
            loaders += [(lambda c=c: k.dma("pool", wd[:, c, :], vd[:, c, :], wdb, writes=[wdb],
                                           max_dma_last_dim=4096)) for c in range(NFC)]
            pre["wd"] = (wd, wdb)
        return pre, loaders

    def phase_ffn(layer, hsrc, hdst, pre=None):
        TS, NTS, NST = 256, 2, 16
        pre = pre or {}
        with Phase(k) as P:
            ublocks = []
            for i_ in range(6):
                wcols = min(512, DFF - i_ * 512)
                ublocks += [(i_ * 512, wcols), (DFF + i_ * 512, wcols)]
            wub_l = load_w_blocks(P, "wup", ffn_w_up[layer], 8, ublocks)
            wd, wdb = pre["wd"] if "wd" in pre else load_w(P, "wdn", ffn_w_down[layer], NFC, D)
            fr = Front(P, hsrc, g_ffn_pre[layer], 7, hb_n=1, nt=NTS, halo=2, use_pool=True)
            post = Post(P, g_ffn_post[layer], ntmp=1, use_pool=True)
            cw, cwb = P.T("cw", [128, 3, 44], F32, dma=True)
            cb, cbb = P.T("cb", [128, 44], F32, dma=True)
            k.dma("sp", cw[:], ffn_cw[layer], cwb, writes=[cwb])
            k.dma("sp", cb[:], ffn_cb[layer], cbb, writes=[cbb])
            cv = [P.T("cv", [128, TS], F32) for _ in range(6)]
            gl = [P.T("gl", [128, TS], F32) for _ in range(3)]
            actTs = [P.T("actT", [128, NFC, TS], BF16) for _ in range(2)]
            hr = [P.T("hr", [128, D], F32, dma=True) for _ in range(2)]
            hsrc_t = hsrc.rearrange("(t p) n -> p t n", p=128)
            hdst_t = hdst.rearrange("(t p) n -> p t n", p=128)
            fr.load(0)
            bc = [0]
            uc = 0
            hrc = [0]
            events = {}

            def at(step, fn):
                events.setdefault(step, []).append(fn)

            def run_events(step):
                for fn in events.pop(step, []):
                    fn()

            class Down:
                def __init__(self, st_, actT, actTb):
                    self.st, self.actT, self.actTb, self.m, self.banks = st_, actT, actTb, 0, {}

                def emit(self, step, n):
                    for _ in range(n):
                        if self.m >= 4 * NFC:
                            return
                        gi, jc = self.m // NFC, self.m % NFC
                        t, hf = gi // 2, gi % 2
                        if jc == 0:
                            self.banks[gi] = 4 + bc[0] % 3
                            bc[0] += 1
                        bk = self.banks[gi]
                        actT, actTb = self.actT, self.actTb
                        k.op("pe", lambda e: e.matmul(ps[:, bk, :], lhsT=actT[:, jc, t * 128:(t + 1) * 128],
                                                      rhs=wd[:, jc, hf * 512:(hf + 1) * 512],
                                                      start=(jc == 0), stop=(jc == NFC - 1)),
                             [wdb, actTb], [PB[bk]])
                        self.m += 1
                        if jc == NFC - 1 and hf == 1:
                            self.schedule_post(step, t, (self.banks[gi - 1], self.banks[gi]))

                def schedule_post(self, step, t, banks):
                    tile = self.st * NTS + t
                    h_t, h_b = hr[hrc[0] % 2]
                    hrc[0] += 1
                    ctx = {}

                    def s_a():
                        k.dma("sp", h_t[:], hsrc_t[:, tile, :], h_b, writes=[h_b])
                        ctx["c"] = post.stage_a(banks)

                    def s_b():
                        post.stage_b(ctx["c"])

                    def s_c():
                        post.stage_c(ctx["c"], h_t[:], h_b)
                        k.dma("sp", hdst_t[:, tile, :], h_t[:], h_b, reads=[h_b])

                    at(step + 1, s_a)
                    at(step + 3, s_b)
                    at(step + 4, s_c)

            dn = None
            cur = fr.emit(0)
            for st in range(NST):
                hnT, hnTb, _, _ = cur
                actT, actTb = actTs[st % 2]
                nxt = st + 1 < NST
                for jc in range(NFC):
                    step = st * NFC + jc
                    if dn is not None:
                        dn.emit(step, 4)
                    run_events(step)
                    if nxt:
                        if jc == 2:
                            fr.load(st + 1)
                        if jc == 5:
                            fr.tr_begin(st + 1)
                        if jc == 6:
                            fr.norm_a(st + 1, 0)
                        if jc == 8:
                            fr.norm_a(st + 1, 1)
                            fr.norm_b(st + 1, 0)
                        if jc == 9:
                            fr.norm_c(st + 1, 0)
                        if jc == 10:
                            fr.norm_b(st + 1, 1)
                        if jc == 12:
                            fr.norm_c(st + 1, 1)
                        if jc == 15:
                            fr.tr_tile(st + 1, 0)
                        if jc == 18:
                            cur = fr.tr_tile(st + 1, 1)
                    pair = []
                    for gv in range(2):
                        bk = (uc % 4)
                        uc += 1
                        ch = gv * NFC + jc
                        wu, wub = wub_l[(jc // 4) * 2 + gv]
                        col = (jc % 4) * 128
                        for c in range(8):
                            k.op("pe", lambda e: e.matmul(ps[:, bk, 0:TS + 2], lhsT=wu[:, c, col:col + 128],
                                                          rhs=hnT[:, c, :], start=(c == 0), stop=(c == 7)),
                                 [wub, hnTb], [PB[bk]])
                        c_t, c_b = cv[uc % len(cv)]
                        k.op("act", lambda e: e.activation(out=c_t[:], in_=ps[:, bk, 2:TS + 2], func=AF.Identity,
                                                           scale=cw[:, 2, ch:ch + 1], bias=cb[:, ch:ch + 1]),
                             [PB[bk], cwb, cbb], [c_b])
                        for kk in (1, 0):
                            k.op("dve", lambda e: e.scalar_tensor_tensor(out=c_t[:], in0=ps[:, bk, kk:kk + TS],
                                                                         scalar=cw[:, kk, ch:ch + 1], in1=c_t[:],
                                                                         op0=ALU.mult, op1=ALU.add),
                                 [PB[bk], cwb, c_b], [c_b])
                        pair.append((c_t, c_b))
                    g_t, g_b = gl[jc % len(gl)]
                    k.op("act", lambda e: e.activation(out=g_t[:], in_=pair[0][0][:], func=AF.Gelu_apprx_tanh),
                         [pair[0][1]], [g_b])
                    k.op("pool", lambda e: e.tensor_tensor(out=actT[:, jc, :], in0=g_t[:], in1=pair[1][0][:],
                                                           op=ALU.mult), [g_b, pair[1][1]], [actTb])
                dn = Down(st, actT, actTb)
            step = NST * NFC
            while dn.m < 4 * NFC or events:
                dn.emit(step, 4)
                run_events(step)
                step += 1

    def phase_ret_proj(layer, hsrc):
        j = layer // 2
        with Phase(k) as P:
            wblk = load_w_blocks(P, "win", ret_w_in[j], 8, [(i_ * D, D) for i_ in range(6)])
            fr = Front(P, hsrc, g_mix_pre[layer], 7)
            qst = P.T("rqst", [128, 8, 512], BF16, dma=True)
            kst = P.T("rkst", [128, 8, 512], BF16, dma=True)
            vst = [P.T("rvst", [128, 2048], BF16, dma=True) for _ in range(2)]
            gst = [P.T("rgst", [128, 2048], BF16, dma=True) for _ in range(2)]
            cs_t, cs_b = P.T("cos", [128, 512], F32, dma=True)
            sn_t, sn_b = P.T("sin", [128, 512], F32, dma=True)
            tm = [P.T("rtm", [128, 512], F32) for _ in range(4)]
            fr.load(0)
            bc = 0
            ev = 0
            for st in range(8):
                cs = slice(st * 512, (st + 1) * 512)
                k.dma("sp", cs_t[:], c_cosT[:, cs], cs_b, writes=[cs_b])
                k.dma("sp", sn_t[:], c_sinT[:, cs], sn_b, writes=[sn_b])
                if st == 0:
                    cur = fr.emit(0)
                    fr.load(1)
                hnT, hnTb, _, _ = cur
                for which, ((s_t, s_b), dst) in enumerate(((qst, qT_d), (kst, kT_d))):
                    for h in range(4):
                        bks = []
                        for dc in range(2):
                            bk = bc % 6
                            bc += 1
                            bks.append(bk)
                            w, wb = wblk[which]
                            col = h * 256 + dc * 128
                            for c in range(8):
                                k.op("pe", lambda e: e.matmul(ps[:, bk, :], lhsT=w[:, c, col:col + 128],
                                                              rhs=hnT[:, c, :], start=(c == 0), stop=(c == 7)),
                                     [wb, hnTb], [PB[bk]])
                        x1, x2 = bks
                        for ti, (xb_, tab, tabb) in enumerate(((x1, cs_t, cs_b), (x2, sn_t, sn_b),
                                                               (x1, sn_t, sn_b), (x2, cs_t, cs_b))):
                            k.op("dve", lambda e: e.tensor_tensor(out=tm[ti][0][:], in0=ps[:, xb_, :], in1=tab[:],
                                                                  op=ALU.mult), [PB[xb_], tabb], [tm[ti][1]])
                        k.op("pool", lambda e: e.tensor_tensor(out=s_t[:, h * 2, :], in0=tm[0][0][:], in1=tm[1][0][:],
                                                               op=ALU.subtract), [tm[0][1], tm[1][1]], [s_b])
                        k.op("pool", lambda e: e.tensor_tensor(out=s_t[:, h * 2 + 1, :], in0=tm[2][0][:],
                                                               in1=tm[3][0][:], op=ALU.add),
                             [tm[2][1], tm[3][1]], [s_b])
                    k.dma("sp", dst.rearrange("h p t -> p h t")[:, :, cs], s_t[:], s_b, reads=[s_b])
                    if st + 1 < 8 and which == 0:
                        fr.emit_norm(st + 1)
                if st + 1 < 8:
                    cur = fr.emit_tr(st + 1)
                    if st + 2 < 8:
                        fr.load(st + 2)
                for t in range(4):
                    row = st * 4 + t
                    for which, (stg, dst) in enumerate(((vst, v_d), (gst, g_d))):
                        s_t, s_b = stg[row % 2]
                        for h in range(4):
                            bk = bc % 6
                            bc += 1
                            w, wb = wblk[2 + which * 2 + h // 2]
                            col = (h % 2) * 512
                            for c in range(8):
                                k.op("pe", lambda e: e.matmul(ps[:, bk, :], lhsT=hnT[:, c, t * 128:(t + 1) * 128],
                                                              rhs=w[:, c, col:col + 512],
                                                              start=(c == 0), stop=(c == 7)), [wb, hnTb], [PB[bk]])
                            if which == 1:
                                k.op("act", lambda e: e.activation(out=s_t[:, h * 512:(h + 1) * 512],
                                                                   in_=ps[:, bk, :], func=AF.Silu), [PB[bk]], [s_b])
                            else:
                                k.op("dve", lambda e: e.tensor_copy(out=s_t[:, h * 512:(h + 1) * 512],
                                                                    in_=ps[:, bk, :]), [PB[bk]], [s_b])
                        k.dma("sp", dst[row * 128:(row + 1) * 128, :], s_t[:], s_b, reads=[s_b])

    def phase_ret_rec(layer):
        j = layer // 2
        with Phase(k) as P:
            dpt, dptb = P.T("dpt", [128, 512], F32, dma=True)
            rc, rcb = P.T("rcol", [128, 16], F32, dma=True)
            gn, gnb = P.T("gncol", [128, 16], F32, dma=True)
            k.dma("sp", dpt[:], c_dpt[:, :], dptb, writes=[dptb])
            k.dma("sp", rc[:], c_rcol[:, :], rcb, writes=[rcb])
            k.dma("sp", gn[:], ret_gn[j], gnb, writes=[gnb])
            gnx, gnxb = P.T("gnx", [128, 16, 128], F32)
            k.op("pool", lambda e: e.memset(gnx[:], 1.0), [], [gnxb])
            for q_ in range(16):
                k.op("pool", lambda e: e.tensor_scalar_mul(out=gnx[:, q_, :], in0=gnx[:, q_, :],
                                                           scalar1=gn[:, q_:q_ + 1]), [gnb, gnxb], [gnxb])
            qg = [P.T("qg", [128, 8, 512], BF16, dma=True) for _ in range(2)]
            kg = [P.T("kg", [128, 8, 512], BF16, dma=True) for _ in range(2)]
            vg = [P.T("vg", [128, 4, 2048], BF16, dma=True) for _ in range(2)]
            gg = [P.T("gg", [128, 4, 2048], BF16, dma=True) for _ in range(2)]
            yT = [P.T("ryT", [128, 16, 512], BF16, dma=True) for _ in range(2)]
            St = [P.T("St", [128, 2, 512], F32) for _ in range(4)]
            Sb = [P.T("Sb", [128, 2, 512], BF16) for _ in range(4)]
            PT = [P.T("PT", [128, 128], BF16) for _ in range(4)]
            kt = [P.T("ktok", [128, 256], BF16) for _ in range(4)]
            NY = 4
            y1 = [P.T("y1", [128, 512], F32) for _ in range(NY)]
            yk = [P.T("ytok", [128, 512], BF16) for _ in range(NY)]
            st6 = [P.T("st6", [128, 1, 6], F32) for _ in range(NY)]
            mv = [P.T("mv", [128, 8], F32) for _ in range(NY)]

            def load(g):
                cs = slice(g * 512, (g + 1) * 512)
                for (t_, b_), src in ((qg[g % 2], qT_d), (kg[g % 2], kT_d)):
                    k.dma("sp", t_[:], src.rearrange("h p t -> p h t")[:, :, cs], b_, writes=[b_])
                for (t_, b_), src in ((vg[g % 2], v_d), (gg[g % 2], g_d)):
                    k.dma("sp", t_[:], src.rearrange("(t p) n -> p t n", p=128)[:, g * 4:(g + 1) * 4, :], b_,
                          writes=[b_])

            load(0)
            oc = 0
            sc = 0
            pend = []
            kt7 = psbf(7)
            kt1 = psbf(1)

            def flush_one():
                if pend:
                    fn, after = pend.pop(0)
                    fn()
                    if after is not None:
                        after()

            for g in range(8):
                if g + 1 < 8:
                    load(g + 1)
                q_t, q_b = qg[g % 2]
                k_t, k_b = kg[g % 2]
                v_t, v_b = vg[g % 2]
                g_t, g_b = gg[g % 2]
                y_t, y_b = yT[g % 2]
                for t in range(4):
                    T_ = g * 4 + t
                    tc = slice(t * 128, (t + 1) * 128)
                    last = (T_ == NT - 1)
                    for h in range(4):
                        for c in range(2):
                            k.op("pe", lambda e: e.matmul(ps[:, 0, h * 128:(h + 1) * 128], lhsT=k_t[:, h * 2 + c, tc],
                                                          rhs=q_t[:, h * 2 + c, tc], start=(c == 0), stop=(c == 1)),
                                 [k_b, q_b], [PB[0]])
                    if not last:
                        for h in range(4):
                            for c in range(2):
                                k.op("pe", lambda e: e.transpose(out=kt1[:, (h * 2 + c) * 128:(h * 2 + c + 1) * 128],
                                                                 in_=k_t[:, h * 2 + c, tc], identity=ident[:]),
                                     [k_b, identB], [PB[1]])
                    for h in range(4):
                        k.op("dve", lambda e: e.tensor_tensor(out=PT[h][0][:], in0=ps[:, 0, h * 128:(h + 1) * 128],
                                                              in1=dpt[:, h * 128:(h + 1) * 128], op=ALU.mult),
                             [PB[0], dptb], [PT[h][1]])
                        if not last:
                            k.op("act", lambda e: e.activation(out=kt[h][0][:], in_=kt1[:, h * 256:(h + 1) * 256],
                                                               func=AF.Identity, scale=rc[:, 8 + h:9 + h]),
                                 [PB[1], rcb], [kt[h][1]])
                    for h in range(4):
                        ob = 2 + oc % 3
                        oc += 1
                        vs = v_t[:, t, h * 512:(h + 1) * 512]
                        k.op("pe", lambda e: e.matmul(ps[:, ob, :], lhsT=PT[h][0][:], rhs=vs, start=True,
                                                      stop=(T_ == 0)), [PT[h][1], v_b], [PB[ob]])
                        if T_ > 0:
                            for c in range(2):
                                k.op("pe", lambda e: e.matmul(ps[:, ob, :], lhsT=q_t[:, h * 2 + c, tc],
                                                              rhs=Sb[h][0][:, c, :], start=False, stop=(c == 1)),
                                     [q_b, Sb[h][1]], [PB[ob]])
                        if not last:
                            for c in range(2):
                                k.op("pe", lambda e: e.matmul(ps[:, 5 + c, :], lhsT=kt[h][0][:, c * 128:(c + 1) * 128],
                                                              rhs=vs, start=True, stop=True),
                                     [kt[h][1], v_b], [PB[5 + c]])
                        s6, s6b = st6[sc % NY]
                        m, mb = mv[sc % NY]
                        y1t, y1b = y1[sc % NY]
                        ykt, ykb = yk[sc % NY]
                        sc += 1
                        k.op("dve", lambda e: e.bn_stats(out=s6[:, 0, :], in_=ps[:, ob, :]), [PB[ob]], [s6b])
                        k.op("dve", lambda e: e.bn_aggr(out=m[:, 0:2], in_=s6[:]), [s6b], [mb])
                        k.op("pool", lambda e: e.tensor_scalar(out=m[:, 2:3], in0=m[:, 1:2], scalar1=rc[:, 4 + h:5 + h],
                                                               scalar2=GN_EPS, op0=ALU.mult, op1=ALU.add),
                             [mb, rcb], [mb])
                        k.op("pool", lambda e: e.tensor_tensor(out=m[:, 4:5], in0=m[:, 2:3], in1=nhalf[:],
                                                               op=ALU.pow), [mb, nhalfB], [mb])
                        k.op("pool", lambda e: e.tensor_tensor(out=m[:, 5:6], in0=m[:, 4:5], in1=rc[:, h:h + 1],
                                                               op=ALU.mult), [mb, rcb], [mb])
                        k.op("pool", lambda e: e.tensor_tensor(out=m[:, 6:7], in0=m[:, 0:1], in1=m[:, 5:6],
                                                               op=ALU.mult), [mb], [mb])
                        k.op("pool", lambda e: e.tensor_scalar_mul(out=m[:, 6:7], in0=m[:, 6:7], scalar1=-1.0),
                             [mb], [mb])
                        k.op("act", lambda e: e.activation(out=y1t[:], in_=ps[:, ob, :], func=AF.Identity,
                                                           scale=m[:, 5:6], bias=m[:, 6:7]), [PB[ob], mb], [y1b])
                        k.op("pool", lambda e: e.tensor_tensor(out=ykt[:], in0=y1t[:],
                                                               in1=g_t[:, t, h * 512:(h + 1) * 512], op=ALU.mult),
                             [y1b, g_b], [ykb])
                        if not last:
                            if T_ == 0:
                                k.op("dve", lambda e: e.tensor_copy(out=St[h][0][:], in_=ps[:, 5:7, :]),
                                     [PB[5], PB[6]], [St[h][1]])
                            else:
                                k.op("dve", lambda e: e.scalar_tensor_tensor(out=St[h][0][:], in0=St[h][0][:],
                                                                             scalar=_SDEC[h], in1=ps[:, 5:7, :],
                                                                             op0=ALU.mult, op1=ALU.add),
                                     [PB[5], PB[6], St[h][1]], [St[h][1]])
                            k.op("act", lambda e: e.copy(out=Sb[h][0][:], in_=St[h][0][:]),
                                 [St[h][1]], [Sb[h][1]])

                        def later(h=h, ykt=ykt, ykb=ykb, y_t=y_t, y_b=y_b, tc=tc):
                            for q in range(4):
                                k.op("pe", lambda e: e.transpose(out=kt7[:, q * 128:(q + 1) * 128],
                                                                 in_=ykt[:, q * 128:(q + 1) * 128], identity=ident[:]),
                                     [ykb, identB], [PB[7]])
                            k.op("dve", lambda e: e.tensor_tensor(
                                out=y_t[:, h * 4:(h + 1) * 4, tc],
                                in0=kt7[:, 0:512].rearrange("p (q n) -> p q n", q=4),
                                in1=gnx[:, h * 4:(h + 1) * 4, :], op=ALU.mult), [PB[7], gnxb], [y_b])

                        after = None
                        if t == 3 and h == 3:
                            after = (lambda g=g, y_t=y_t, y_b=y_b:
                                     k.dma("sp", oT_d[g], y_t[:], y_b, reads=[y_b]))
                        pend.append((later, after))
                        if len(pend) > 2:
                            flush_one()
                if g == 7:
                    while pend:
                        flush_one()

    hcur = x
    for layer in layers:
        last = layer == layers[-1]
        if layer % 2 == 0:
            if want():
                phase_sb_proj(layer, hcur)
            if want():
                phase_sb_attn()
            w3, kc3, wdn_pre = sb_w_o[layer // 2], 8, True
        else:
            if want():
                phase_ret_proj(layer, hcur)
            if want():
                phase_ret_rec(layer)
            w3, kc3, wdn_pre = ret_w_o[layer // 2], 16, False
        do3, do4 = want(), want()
        with Phase(k) as PW:
            pre, loaders = ffn_weight_loaders(PW, layer, wdn_pre) if (do4 and PREFETCH_FFN) else ({}, [])
            if do3:
                phase_outproj(layer, w3, kc3, hcur, hs, background=loaders)
            else:
                for f_ in loaders:
                    f_()
            hcur = hs
            if do4:
                phase_ffn(layer, hs, y if last else hs, pre=pre)
    k.barrier([identB])
    k.nc_ref = nc
    return nc, k


def _host_inputs(inputs):
    c, _ = _consts()
    f = lambda a: np.ascontiguousarray(np.asarray(a, dtype=np.float32))
    shared = {
        "norm_mix_pre": f(inputs["norm_mix_pre"]), "norm_mix_post": f(inputs["norm_mix_post"]),
        "norm_ffn_pre": f(inputs["norm_ffn_pre"]), "norm_ffn_post": f(inputs["norm_ffn_post"]),
        "sb_w_qkv": f(inputs["sb_w_qkv"]), "sb_w_o": f(inputs["sb_w_o"]),
        "ret_w_in": f(inputs["ret_w_in"]), "ret_w_o": f(inputs["ret_w_o"]),
        "ret_gn": np.ascontiguousarray(f(inputs["ret_gn"]).reshape(2, 16, 128).transpose(0, 2, 1)),
        "ffn_w_up": f(inputs["ffn_w_up"]), "ffn_w_down": f(inputs["ffn_w_down"]),
        "ffn_cw": np.ascontiguousarray(f(inputs["ffn_conv_w"]).reshape(DEPTH, 3, 44, 128).transpose(0, 3, 1, 2)),
        "ffn_cb": np.ascontiguousarray(f(inputs["ffn_conv_b"]).reshape(DEPTH, 44, 128).transpose(0, 2, 1)),
        "c_ident": c["ident"], "c_negtri": c["negtri"], "c_selk": c["selk"], "c_zero": c["zero"], "c_esel": c["esel"],
        "c_masks": c["masks"], "c_cosT": c["cosT"], "c_sinT": c["sinT"], "c_dpt": c["dpt"], "c_rcol": c["rcol"],
    }
    return shared


def kernel(**inputs):
    xs = np.asarray(inputs["x"], dtype=np.float32)
    shared = _host_inputs(inputs)
    nc, _ = build()
    in_maps = [dict(shared, x=np.ascontiguousarray(xs[b])) for b in range(8)]
    res = run_bass_kernel_spmd(nc, in_maps, core_ids=list(range(8)))
    return np.stack([np.asarray(r["y"], dtype=np.float32) for r in res.results], axis=0)
```
